# Optimizing a Trainium2 kernel written in Bass

```python
import math
import jax, jax.numpy as jnp
from jax import lax
import numpy as np

D_MODEL = 1024
BATCH = 4
SEQ = 8192
DEPTH = 4

CHUNK = 64
N_A_LAYERS = DEPTH // 2
N_B_LAYERS = DEPTH - N_A_LAYERS

A_HEADS = 8
A_HEAD_K = 128
A_HEAD_V = 128
A_QK_WIDTH = A_HEADS * A_HEAD_K
A_V_WIDTH = A_HEADS * A_HEAD_V
A_CONV = 4
A_CONV_WIDTH = 2 * A_QK_WIDTH + A_V_WIDTH
A_IN_WIDTH = 2 * A_QK_WIDTH + 2 * A_V_WIDTH + 2 * A_HEADS

B_HEADS = 16
B_HEAD_DIM = 64
B_WIDTH = B_HEADS * B_HEAD_DIM
LEFT_CHUNKS = 8
BAND = (LEFT_CHUNKS + 1) * CHUNK
REL_CLIP = 256

FFN_DIM = 2816
FFN_CONV = 3

EPS = 1e-6
NEG_INF = -1e30

kernel_name = "yoco_gdn_chunkattn_convffn"


def rmsnorm(x, g):
    xf = x.astype(jnp.float32)
    y = xf * lax.rsqrt(jnp.mean(xf * xf, axis=-1, keepdims=True) + EPS)
    return (y * g.astype(jnp.float32)).astype(x.dtype)


def causal_dwconv(x, w):
    width = w.shape[0]
    return lax.conv_general_dilated(
        x, w[:, None, :].astype(x.dtype), window_strides=(1,), padding=[(width - 1, 0)],
        dimension_numbers=("NWC", "WIO", "NWC"), feature_group_count=x.shape[-1])


def _l2norm(t):
    return t * lax.rsqrt(jnp.sum(t * t, axis=-1, keepdims=True) + EPS)


def chunk_gated_delta_rule(q, k, v, beta, g):
    bsz, seq, nh, dk = q.shape
    dv = v.shape[-1]
    nc = seq // CHUNK

    def chunks(t):
        return t.reshape(bsz, nc, CHUNK, nh, -1).transpose(0, 3, 1, 2, 4)

    q, k, v = chunks(q), chunks(k), chunks(v)
    beta = chunks(beta[..., None])[..., 0]
    gcum = jnp.cumsum(chunks(g[..., None])[..., 0], axis=-1)

    causal = jnp.tril(jnp.ones((CHUNK, CHUNK), dtype=bool))
    strict = jnp.tril(jnp.ones((CHUNK, CHUNK), dtype=bool), k=-1)
    diff = gcum[..., :, None] - gcum[..., None, :]
    decay = jnp.where(causal, jnp.exp(jnp.where(causal, diff, 0.0)), 0.0)

    k_beta = k * beta[..., None]
    m = jnp.where(strict, jnp.einsum("bhnid,bhnjd->bhnij", k_beta, k) * decay, 0.0)
    eye = jnp.eye(CHUNK, dtype=m.dtype)
    rhs = jnp.concatenate([v * beta[..., None], k_beta * jnp.exp(gcum)[..., None]], axis=-1)
    uw = lax.linalg.triangular_solve(m + eye, rhs, left_side=True, lower=True, unit_diagonal=True)
    u, w = uw[..., :dv], uw[..., dv:]

    attn_qk = jnp.einsum("bhnid,bhnjd->bhnij", q, k) * decay
    q_dec = q * jnp.exp(gcum)[..., None]
    k_end = k * jnp.exp(gcum[..., -1:] - gcum)[..., None]
    chunk_decay = jnp.exp(gcum[..., -1])

    xs = tuple(jnp.moveaxis(t, 2, 0) for t in (q_dec, k_end, u, w, attn_qk, chunk_decay))

    def step(state, inp):
        qd, ke, u_c, w_c, a_c, dec = inp
        v_new = u_c - jnp.einsum("bhcd,bhdv->bhcv", w_c, state)
        o_c = jnp.einsum("bhcd,bhdv->bhcv", qd, state) + jnp.einsum("bhcj,bhjv->bhcv", a_c, v_new)
        state = state * dec[..., None, None] + jnp.einsum("bhcd,bhcv->bhdv", ke, v_new)
        return state, o_c

    state0 = jnp.zeros((bsz, nh, dk, dv), jnp.float32)
    _, o = lax.scan(step, state0, xs)
    return o.transpose(1, 0, 3, 2, 4).reshape(bsz, seq, nh, dv)


def gated_deltanet(xn, w_in, conv_w, A_log, dt_bias, out_norm_w, w_out):
    bsz, seq, _ = xn.shape
    proj = xn @ w_in
    qkv = jax.nn.silu(causal_dwconv(proj[..., :A_CONV_WIDTH], conv_w))
    z = proj[..., A_CONV_WIDTH:A_CONV_WIDTH + A_V_WIDTH]
    b_raw = proj[..., A_CONV_WIDTH + A_V_WIDTH:A_CONV_WIDTH + A_V_WIDTH + A_HEADS]
    a_raw = proj[..., A_CONV_WIDTH + A_V_WIDTH + A_HEADS:]
    f32 = jnp.float32
    q = _l2norm(qkv[..., :A_QK_WIDTH].reshape(bsz, seq, A_HEADS, A_HEAD_K).astype(f32)) * (A_HEAD_K ** -0.5)
    k = _l2norm(qkv[..., A_QK_WIDTH:2 * A_QK_WIDTH].reshape(bsz, seq, A_HEADS, A_HEAD_K).astype(f32))
    v = qkv[..., 2 * A_QK_WIDTH:].reshape(bsz, seq, A_HEADS, A_HEAD_V).astype(f32)
    beta = jax.nn.sigmoid(b_raw.astype(f32))
    g = -jnp.exp(A_log.astype(f32)) * jax.nn.softplus(a_raw.astype(f32) + dt_bias.astype(f32))
    o = chunk_gated_delta_rule(q, k, v, beta, g).astype(xn.dtype)
    o = rmsnorm(o, out_norm_w) * jax.nn.silu(z.reshape(bsz, seq, A_HEADS, A_HEAD_V))
    return o.reshape(bsz, seq, A_V_WIDTH) @ w_out


def chunk_attention(xn, w_q, rel_bias, w_out, k_pad, v_pad):
    bsz, seq, _ = xn.shape
    nc = seq // CHUNK
    q = (xn @ w_q).reshape(bsz, nc, CHUNK, B_HEADS, B_HEAD_DIM).transpose(1, 0, 2, 3, 4)
    rel = jnp.arange(CHUNK)[:, None] + LEFT_CHUNKS * CHUNK - jnp.arange(BAND)[None, :]
    bias = rel_bias[:, jnp.clip(rel, -REL_CLIP, REL_CLIP) + REL_CLIP].astype(jnp.float32)
    scale = B_HEAD_DIM ** -0.5

    def one_chunk(args):
        n, q_n = args
        k_band = lax.dynamic_slice_in_dim(k_pad, n * CHUNK, BAND, axis=1)
        v_band = lax.dynamic_slice_in_dim(v_pad, n * CHUNK, BAND, axis=1)
        s = jnp.einsum("bqhd,bkhd->bhqk", q_n, k_band).astype(jnp.float32) * scale + bias
        valid = jnp.arange(BAND) >= (LEFT_CHUNKS - n) * CHUNK
        s = jnp.where(valid, s, NEG_INF)
        p = jax.nn.softmax(s, axis=-1).astype(v_band.dtype)
        return jnp.einsum("bhqk,bkhd->bqhd", p, v_band)

    o = lax.map(one_chunk, (jnp.arange(nc, dtype=jnp.int32), q))
    o = o.transpose(1, 0, 2, 3, 4).reshape(bsz, seq, B_WIDTH)
    return o @ w_out


def conv_ffn(xn, w_up, conv_w, conv_b, w_down):
    h = causal_dwconv(xn @ w_up, conv_w) + conv_b
    gate, val = h[..., :FFN_DIM], h[..., FFN_DIM:]
    return (jax.nn.silu(gate) * val) @ w_down


def setup_inputs(seed: int = 0) -> dict:
    key = jax.random.key(seed)
    ks = jax.random.split(key, 22)

    def nrm(k, shape, scale):
        return jax.random.normal(k, shape, jnp.float32) * scale

    dt = jnp.exp(jax.random.uniform(ks[5], (N_A_LAYERS, A_HEADS), jnp.float32,
                                    minval=math.log(1e-3), maxval=math.log(1e-1)))
    return {
        "x": nrm(ks[0], (BATCH, SEQ, D_MODEL), 1.0),
        "a_norm": 1.0 + nrm(ks[1], (N_A_LAYERS, D_MODEL), 0.02),
        "a_w_in": nrm(ks[2], (N_A_LAYERS, D_MODEL, A_IN_WIDTH), D_MODEL ** -0.5),
        "a_conv": nrm(ks[3], (N_A_LAYERS, A_CONV, A_CONV_WIDTH), A_CONV ** -0.5),
        "a_A_log": jnp.log(jax.random.uniform(ks[4], (N_A_LAYERS, A_HEADS), jnp.float32, minval=1.0, maxval=16.0)),
        "a_dt_bias": dt + jnp.log(-jnp.expm1(-dt)),
        "a_out_norm": 1.0 + nrm(ks[6], (N_A_LAYERS, A_HEAD_V), 0.02),
        "a_w_out": nrm(ks[7], (N_A_LAYERS, A_V_WIDTH, D_MODEL), A_V_WIDTH ** -0.5),
        "kv_norm": 1.0 + nrm(ks[8], (D_MODEL,), 0.02),
        "w_kv": nrm(ks[9], (D_MODEL, 2 * B_WIDTH), D_MODEL ** -0.5),
        "b_norm": 1.0 + nrm(ks[10], (N_B_LAYERS, D_MODEL), 0.02),
        "b_w_q": nrm(ks[11], (N_B_LAYERS, D_MODEL, B_WIDTH), D_MODEL ** -0.5),
        "b_rel_bias": nrm(ks[12], (N_B_LAYERS, B_HEADS, 2 * REL_CLIP + 1), 0.1),
        "b_w_out": nrm(ks[13], (N_B_LAYERS, B_WIDTH, D_MODEL), B_WIDTH ** -0.5),
        "f_norm": 1.0 + nrm(ks[14], (DEPTH, D_MODEL), 0.02),
        "f_w_up": nrm(ks[15], (DEPTH, D_MODEL, 2 * FFN_DIM), D_MODEL ** -0.5),
        "f_conv": nrm(ks[16], (DEPTH, FFN_CONV, 2 * FFN_DIM), FFN_CONV ** -0.5),
        "f_conv_b": nrm(ks[17], (DEPTH, 2 * FFN_DIM), 0.01),
        "f_w_down": nrm(ks[18], (DEPTH, FFN_DIM, D_MODEL), FFN_DIM ** -0.5),
        "final_norm": 1.0 + nrm(ks[19], (D_MODEL,), 0.02),
    }


def reference(x, a_norm, a_w_in, a_conv, a_A_log, a_dt_bias, a_out_norm, a_w_out,
              kv_norm, w_kv, b_norm, b_w_q, b_rel_bias, b_w_out,
              f_norm, f_w_up, f_conv, f_conv_b, f_w_down, final_norm):
    bsz, seq, _ = x.shape
    h = x
    k_pad = None
    v_pad = None
    for layer in range(DEPTH):
        if layer < N_A_LAYERS:
            i = layer
            h = h + gated_deltanet(rmsnorm(h, a_norm[i]), a_w_in[i], a_conv[i], a_A_log[i],
                                   a_dt_bias[i], a_out_norm[i], a_w_out[i])
        else:
            if layer == N_A_LAYERS:
                kv = rmsnorm(h, kv_norm) @ w_kv
                pad = ((0, 0), (LEFT_CHUNKS * CHUNK, 0), (0, 0), (0, 0))
                k_pad = jnp.pad(kv[..., :B_WIDTH].reshape(bsz, seq, B_HEADS, B_HEAD_DIM), pad)
                v_pad = jnp.pad(kv[..., B_WIDTH:].reshape(bsz, seq, B_HEADS, B_HEAD_DIM), pad)
            j = layer - N_A_LAYERS
            h = h + chunk_attention(rmsnorm(h, b_norm[j]), b_w_q[j], b_rel_bias[j], b_w_out[j], k_pad, v_pad)
        h = h + conv_ffn(rmsnorm(h, f_norm[layer]), f_w_up[layer], f_conv[layer], f_conv_b[layer], f_w_down[layer])
    return rmsnorm(h, final_norm)
```

```python
import numpy as np
import concourse.bass as bass
import concourse.mybir as mybir
from concourse.bass_utils import run_bass_kernel_spmd

F32 = mybir.dt.float32
BF16 = mybir.dt.bfloat16
AF = mybir.ActivationFunctionType
ALU = mybir.AluOpType


class Buf:
    __slots__ = ("name", "ap", "w", "r")

    def __init__(self, name, ap):
        self.name = name
        self.ap = ap
        self.w = None
        self.r = {}

    def __getitem__(self, idx):
        return View(self, self.ap[idx])

    @property
    def v(self):
        return View(self, self.ap)


class View:
    __slots__ = ("buf", "ap")

    def __init__(self, buf, ap):
        self.buf = buf
        self.ap = ap

    def __getitem__(self, idx):
        return View(self.buf, self.ap[idx])


class DSem:
    def __init__(self, sem):
        self.sem = sem
        self.count = 0


class Ctx:
    def __init__(self, nc, same_engine_sync=False):
        self.nc = nc
        self.eng = {"pe": nc.tensor, "act": nc.scalar, "dve": nc.vector, "pool": nc.gpsimd, "sync": nc.sync}
        self.sem = {}
        self.waited = {}
        self.same_engine_sync = same_engine_sync
        self.small_thresh = 512
        self.n_ins = 0
        self.n_wait = 0

    def new_epoch(self):
        self.n_epoch = getattr(self, "n_epoch", 0) + 1
        for e in ("pe", "act", "dve", "pool"):
            self.sem[e] = [self.nc.alloc_semaphore(name="s_%s_%d" % (e, self.n_epoch)), 0]

    def dma_sem(self, name=None):
        self.n_dsem = getattr(self, "n_dsem", 0) + 1
        return DSem(self.nc.alloc_semaphore(name="d_%s_%d" % (name, self.n_dsem)))

    def sb(self, name, shape, dtype):
        t = self.nc.alloc_sbuf_tensor(name, shape, dtype)
        return Buf(name, t.ap())

    def ps(self, name, shape, dtype=F32):
        t = self.nc.alloc_psum_tensor(name, shape, dtype)
        return Buf(name, t.ap())

    def wrap(self, name, ap):
        return Buf(name, ap)

    def _wait(self, engname, deps):
        best = {}
        for d in deps:
            if d is None:
                continue
            sem, cnt, src = d[0], d[1], d[2]
            if src == engname and not self.same_engine_sync and not (len(d) > 3 and d[3]):
                continue
            k = id(sem)
            if k not in best or best[k][1] < cnt:
                best[k] = (sem, cnt)
        e = self.eng[engname]
        for k, (sem, cnt) in best.items():
            if self.waited.get((engname, k), 0) >= cnt:
                continue
            e.wait_ge(sem, cnt)
            self.n_wait += 1
            self.waited[(engname, k)] = cnt

    def _deps(self, reads, writes):
        deps = []
        for b in reads:
            deps.append(b.w)
        for b in writes:
            deps.append(b.w)
            deps.extend(b.r.values())
        return deps

    def _record(self, ev, reads, writes):
        k = id(ev[0])
        for b in reads:
            b.r[k] = ev
        for b in writes:
            b.w = ev
            b.r = {}

    def op(self, engname, method, _reads=(), _writes=(), **kw):
        reads, writes = list(_reads), list(_writes)
        args = {}
        for k, v in kw.items():
            if isinstance(v, View):
                (writes if k in ("out", "accum_out", "ap") else reads).append(v.buf)
                args[k] = v.ap
            else:
                args[k] = v
        self._wait(engname, self._deps(reads, writes))
        ins = getattr(self.eng[engname], method)(**args)
        s = self.sem[engname]
        s[1] += 1
        ins.then_inc(s[0], 1)
        self.n_ins += 1
        small = True
        o = kw.get("out", kw.get("ap"))
        if isinstance(o, View):
            shp = o.ap.shape
            n = 1
            for d_ in shp[1:]:
                n *= int(d_)
            small = n < self.small_thresh
        self._record((s[0], s[1], engname, small), reads, writes)
        return ins

    def pe(self, outbufs, instrs, _reads=()):
        reads = list(_reads)
        writes = list(outbufs)
        for o, l, r, st, sp, tr in instrs:
            reads.append(l.buf)
            reads.append(r.buf)
            if o.buf not in writes:
                writes.append(o.buf)
        self._wait("pe", self._deps(list(_reads), writes))
        pe = self.eng["pe"]
        ins = None
        for o, l, r, st, sp, tr in instrs:
            self._wait("pe", [l.buf.w, r.buf.w])
            if tr:
                ins = pe.transpose(o.ap, l.ap, r.ap)
            else:
                ins = pe.matmul(o.ap, l.ap, r.ap, start=st, stop=sp)
            self.n_ins += 1
        s = self.sem["pe"]
        s[1] += 1
        ins.then_inc(s[0], 1)
        self._record((s[0], s[1], "pe"), reads, writes)

    def mm(self, out, pairs):
        n = len(pairs)
        self.pe([out.buf], [(out, l, r, i == 0, i == n - 1, False) for i, (l, r) in enumerate(pairs)])

    def dma(self, qname, out, in_, dsem):
        reads = [in_.buf] if isinstance(in_, View) else []
        writes = [out.buf] if isinstance(out, View) else []
        self._wait(qname, self._deps(reads, writes))
        o = out.ap if isinstance(out, View) else out
        i = in_.ap if isinstance(in_, View) else in_
        ins = self.eng[qname].dma_start(out=o, in_=i)
        dsem.count += 16
        ins.then_inc(dsem.sem, 16)
        self.n_ins += 1
        self._record((dsem.sem, dsem.count, "dma"), reads, writes)

    def finish(self, dsems):
        for d in dsems:
            self.eng["sync"].wait_ge(d.sem, d.count)


T = 512
NT_FULL = 16
EPS = 1e-6
NSLOT = 9
HOLD = 5

N_A, N_F, N_KV, N_B = 41, 66, 16, 26
NSTREAM = 2 * N_A + 4 * N_F + N_KV + 2 * N_B
EB_SLOT = [2 * (N_A + N_F) + N_KV + 8, 2 * (N_A + N_F) + N_KV + N_B + N_F + 8]

V_NORM = 0
V_ACONV = 80
V_FCONV = V_ACONV + 2 * 96
V_FBIAS = V_FCONV + 4 * 132
V_ONW = V_FBIAS + 4 * 44
V_ALOG = V_ONW + 2
V_DTB = V_ALOG + 16
NV = V_DTB + 16
N_ANORM, N_KVNORM, N_BNORM, N_FNORM, N_FINAL = 0, 2, 3, 5, 9

C_ID, C_U, C_SLO, C_SU, C_ONE = 0, 128, 256, 384, 512


def _bc(view, shape, axis):
    return View(view.buf, view.ap.unsqueeze(axis).broadcast_to(list(shape)))


N_NEWTON = 1
DUMP_L = 0
DUMP_HG = 0
ALL_PHASES = ("a0", "f0", "a1", "f1", "kv", "b0", "f2", "b1", "f3", "fin")


def build_program(NT=NT_FULL, debug_h=(), phases=ALL_PHASES, dumps=False):
    nc = bass.Bass("TRN2", target_bir_lowering=False)
    cx = Ctx(nc)
    S = NT * T
    x_d = nc.dram_tensor("x", [S, 1024], F32, kind="ExternalInput").ap()
    wst_d = nc.dram_tensor("wst", [NSTREAM * 128, 1024], F32, kind="ExternalInput").ap()
    relb_d = nc.dram_tensor("relb", [2, 128, 80 * 128], F32, kind="ExternalInput").ap()
    vecs_d = nc.dram_tensor("vecs", [128, NV], F32, kind="ExternalInput").ap()
    cst_d = nc.dram_tensor("cst", [128, 5 * 128], F32, kind="ExternalInput").ap()
    y_d = nc.dram_tensor("y", [S, 1024], F32, kind="ExternalOutput").ap()
    dbg_d = None
    if debug_h:
        dbg_d = nc.dram_tensor("dbg", [len(debug_h), S, 1024], F32, kind="ExternalOutput").ap()
    wbf_d = nc.dram_tensor("wbf", [NSTREAM * 128, 1024], BF16, kind="Internal").ap()

    cx.new_epoch()

    def chunks(name, n, width, dtype):
        t = nc.alloc_sbuf_tensor(name, [128, n, width], dtype)
        full = t.ap()
        return full, [Buf("%s%d" % (name, c), full[:, c, :]) for c in range(n)]

    h_all, h = chunks("h", 8, T, F32)
    xn_all, xn = chunks("xn", 8, T, BF16)
    big_all, big = chunks("big", 24, T, BF16)
    ba_all, b8a = chunks("b8a", 8, T, BF16)
    bb_all, b8b = chunks("b8b", 8, T, BF16)
    kt_all, KT = chunks("kt", 8, 2 * T, BF16)
    vt_all, VT = chunks("vt", 8, 16 * 65, BF16)
    S32, Sbf = [], []
    for l in range(2):
        _, s32 = chunks("s32_%d" % l, 2, 512, F32)
        _, sb_ = chunks("sbf_%d" % l, 2, 512, BF16)
        S32.append(s32)
        Sbf.append(sb_)
    tailA = [chunks("tailA%d" % l, 24, 3, F32)[1] for l in range(2)]
    tailF = [chunks("tailF%d" % l, 44, 2, F32)[1] for l in range(4)]
    cst = cx.sb("cst_sb", [128, 5 * 128], F32)
    cstb = cx.sb("cstb", [128, 2 * 128], BF16)
    vecs = cx.sb("vecs_sb", [128, NV], F32)
    negA = cx.sb("negA", [128, 16], F32)
    ring = [cx.sb("ring%d" % i, [128, 1024], BF16) for i in range(NSLOT)]
    io32 = [cx.sb("io%d" % i, [128, 1024], F32) for i in range(2)]
    xc = [cx.sb("xc%d" % i, [128, T + 3], F32) for i in range(4)]
    acc = [cx.sb("acc%d" % i, [128, T], F32) for i in range(3)]
    tf = [cx.sb("tf%d" % i, [128, T], F32) for i in range(6)]
    tb = [cx.sb("tb%d" % i, [128, T], BF16) for i in range(35)]
    gsm = [cx.sb("gsm%d" % i, [128, 96 if i == 6 else 32], F32) for i in range(7)]
    rdt = cx.sb("rdt", [128, 8], F32)
    TSETS = [(tb[1:18], tf[2:4]), (tb[18:35], tf[4:6])]
    psb = [cx.ps("psb%d" % i, [128, 512], F32) for i in range(6)]
    pst = [cx.ps("pst%d" % i, [128, 1024], BF16) for i in range(2)]
    st = {"ps": 0, "pt": 0, "io": 0, "xc": 0}

    def psum():
        st["ps"] = (st["ps"] + 1) % len(psb)
        return psb[st["ps"]]

    def psumT():
        st["pt"] = (st["pt"] + 1) % len(pst)
        return pst[st["pt"]]

    def vcol(c, n=1):
        return vecs[:, c:c + n]

    ident = cst[:, C_ID:C_ID + 128]
    Umask = cst[:, C_U:C_U + 128]
    SLo = cst[:, C_SLO:C_SLO + 128]
    SU = cst[:, C_SU:C_SU + 128]
    ones = cst[:, C_ONE:C_ONE + 128]
    identb = cstb[:, 0:128]
    onesb = cstb[:, 128:256]

    d_misc = cx.dma_sem("misc")
    cx.dma("sync", cst.v, cst_d, d_misc)
    cx.dma("sync", vecs.v, vecs_d, d_misc)
    cx.op("dve", "tensor_copy", out=cstb[:, 0:128], in_=ident)
    cx.op("dve", "tensor_copy", out=cstb[:, 128:256], in_=ones)
    cx.op("act", "activation", out=negA.v, in_=vecs[:, V_ALOG:V_ALOG + 16], func=AF.Exp)
    cx.op("dve", "tensor_scalar", out=negA.v, in0=negA.v, scalar1=-1.0, scalar2=None, op0=ALU.mult)
    for l in range(2):
        for hg in range(2):
            cx.op("pool", "memset", ap=S32[l][hg].v, constant=0.0)
            cx.op("pool", "memset", ap=Sbf[l][hg].v, constant=0.0)
        for c in range(24):
            cx.op("pool", "memset", ap=tailA[l][c].v, constant=0.0)
    for l in range(4):
        for c in range(44):
            cx.op("pool", "memset", ap=tailF[l][c].v, constant=0.0)
    for bi in range(8):
        cx.op("pool", "memset", ap=VT[bi].v, constant=1.0)

    eb_slots = set()
    for j in range(2):
        for i in range(10):
            eb_slots.add(EB_SLOT[j] + i)
    groups = []
    slot_group = {}
    sidx = 0
    while sidx < NSTREAM:
        if sidx in eb_slots:
            n = 1
        else:
            n = 1
            while n < 8 and sidx + n < NSTREAM and (sidx + n) not in eb_slots:
                n += 1
        gb = Buf("wg%d" % sidx, wbf_d[sidx * 128:(sidx + n) * 128, :])
        groups.append((sidx, n, gb))
        for k in range(n):
            slot_group[sidx + k] = gb
        sidx += n
    NSEG = 6
    cast_groups = [g for g in groups if g[0] not in eb_slots]
    per = (len(cast_groups) + NSEG - 1) // NSEG
    for sg in range(NSEG):
        seg = cast_groups[sg * per:(sg + 1) * per]
        if not seg:
            continue
        d_cast = cx.dma_sem("cast%d" % sg)
        for (s0, n, gb) in seg:
            cx.dma("pool", gb.v, wst_d[s0 * 128:(s0 + n) * 128, :], d_cast)
        for (s0, n, gb) in seg:
            gb.w = (d_cast.sem, d_cast.count, "dma")
    ebt = [Buf("ebt%d" % i, big_all[:, 2 * i:2 * i + 2, :].rearrange("p a b -> p (a b)")) for i in range(2)]
    d_ebl = [cx.dma_sem("ebl%d" % i) for i in range(2)]
    d_ebs = cx.dma_sem("ebs")
    ebg = []
    for j in range(2):
        for i in range(10):
            iob = io32[i % 2]
            cx.dma("sync", iob.v, relb_d[j, :, i * 1024:(i + 1) * 1024], d_ebl[i % 2])
            ebtmp = tb[1 + 2 * (i % 2)] if False else ebt[i % 2]
            cx.op("act", "activation", out=ebtmp.v, in_=iob.v, func=AF.Exp)
            gb = slot_group[EB_SLOT[j] + i]
            cx.dma("sync", gb.v, ebtmp.v, d_ebs)
            ebg.append(gb)
    for gb in ebg:
        gb.w = (d_ebs.sem, d_ebs.count, "dma")

    ring_sem = [cx.dma_sem("ring%d" % i) for i in range(NSLOT)]
    rs = {"load": 0, "use": 0}
    total_uses = NT * NSTREAM

    def ring_prefetch(limit):
        while rs["load"] < min(limit, total_uses):
            L = rs["load"]
            sl = L % NSLOT
            s = L % NSTREAM
            gb = slot_group[s]
            cx.dma("sync", ring[sl].v, View(gb, wbf_d[s * 128:(s + 1) * 128, :]), ring_sem[sl])
            rs["load"] += 1

    def take():
        u = rs["use"]
        ring_prefetch(u + NSLOT - HOLD)
        rs["use"] += 1
        return ring[u % NSLOT]

    def wt(slot, k):
        return slot[:, k * 128:(k + 1) * 128]

    d_dump = cx.dma_sem("dump")
    dump_names = []

    def dump(name, view):
        if not dumps:
            return
        shp = list(view.ap.shape)
        cols = int(np.prod(shp[1:]))
        t = nc.dram_tensor("dd_" + name, [128, cols], F32, kind="ExternalOutput").ap()
        if len(shp) == 3:
            t = t.rearrange("p (a b) -> p a b", a=shp[1])
        cx.dma("pool", t, view, d_dump)
        dump_names.append(name)

    def rmsnorm(norm_idx, dst, f32_dst=None):
        for c in range(8):
            if c % 2 == 0:
                cx.op("act", "activation", out=xn[c].v, in_=h[c].v, func=AF.Square)
            else:
                cx.op("dve", "tensor_tensor", out=xn[c].v, in0=h[c].v, in1=h[c].v, op=ALU.mult)
        ps = psum()
        cx.mm(ps.v, [(onesb, xn[c].v) for c in range(8)])
        cx.op("act", "activation", out=tf[0].v, in_=ps.v, func=AF.Ln, scale=1.0 / 1024.0, bias=EPS)
        cx.op("act", "activation", out=tf[1].v, in_=tf[0].v, func=AF.Exp, scale=-0.5)
        for c in range(8):
            o = (f32_dst[c] if f32_dst is not None else dst[c])
            cx.op("dve", "scalar_tensor_tensor", out=o.v, in0=h[c].v, scalar=vcol(V_NORM + norm_idx * 8 + c),
                  in1=tf[1].v, op0=ALU.mult, op1=ALU.mult)

    def proj(slot, src):
        ps = psum()
        cx.mm(ps.v, [(wt(slot, kc), src[kc].v) for kc in range(8)])
        return ps

    def conv_taps(ps, tail, wbase, ntap, bias_col, accb):
        K = ntap - 1
        st["xc"] = (st["xc"] + 1) % len(xc)
        xb = xc[st["xc"]]
        cx.op("pool", "tensor_copy", out=xb[:, 0:K], in_=tail.v)
        cx.op("act", "activation", out=xb[:, K:K + T], in_=ps.v, func=AF.Copy)
        cx.op("pool", "tensor_copy", out=tail.v, in_=xb[:, T:T + K])
        if bias_col is None:
            cx.op("act", "activation", out=accb.v, in_=xb[:, 0:T], func=AF.Identity, scale=vcol(wbase))
        else:
            cx.op("act", "activation", out=accb.v, in_=xb[:, 0:T], func=AF.Identity, scale=vcol(wbase), bias=vcol(bias_col))
        for k in range(1, ntap):
            cx.op("dve", "scalar_tensor_tensor", out=accb.v, in0=xb[:, k:k + T], scalar=vcol(wbase + k),
                  in1=accb.v, op0=ALU.mult, op1=ALU.add)

    def residual_out(slots_fn, src, nk):
        for oc in range(8):
            ps = psum()
            cx.mm(ps.v, [(slots_fn(oc, kc), src[kc].v) for kc in range(nk)])
            cx.op("dve", "tensor_tensor", out=h[oc].v, in0=ps.v, in1=h[oc].v, op=ALU.add)

    def ffn(l):
        rmsnorm(N_FNORM + l, xn)
        for c in range(22):
            sg = take()
            psg = proj(sg, xn)
            conv_taps(psg, tailF[l][c], V_FCONV + l * 132 + c * 3, 3, V_FBIAS + l * 44 + c, acc[0])
            sv = take()
            psv = proj(sv, xn)
            conv_taps(psv, tailF[l][22 + c], V_FCONV + l * 132 + (22 + c) * 3, 3, V_FBIAS + l * 44 + 22 + c, acc[1])
            cx.op("act", "activation", out=acc[2].v, in_=acc[0].v, func=AF.Silu)
            cx.op("dve", "tensor_tensor", out=big[c].v, in0=acc[1].v, in1=acc[2].v, op=ALU.mult)
        held = {}

        def dtile(oc, kc):
            idx = oc * 22 + kc
            si = idx // 8
            while si not in held:
                held[len(held)] = take()
            return wt(held[si], idx % 8)
        residual_out(dtile, big, 22)
        while len(held) < 22:
            held[len(held)] = take()

    def a_layer(l, ti):
        rmsnorm(N_ANORM + l, xn)
        pend = None
        for oc in range(24):
            sl = take()
            ps = proj(sl, xn)
            conv_taps(ps, tailA[l][oc], V_ACONV + l * 96 + oc * 4, 4, None, acc[oc % 3])
            if pend is not None:
                cx.op("act", "activation", out=big[pend].v, in_=acc[pend % 3].v, func=AF.Silu)
            pend = oc
        cx.op("act", "activation", out=big[pend].v, in_=acc[pend % 3].v, func=AF.Silu)
        for oc in range(8):
            sl = take()
            ps = proj(sl, xn)
            cx.op("act", "activation", out=b8a[oc].v, in_=ps.v, func=AF.Silu)
        def l2_a(c):
            sq = tb[0] if c % 2 == 0 else tb[1]
            if c % 2 == 0:
                cx.op("act", "activation", out=sq.v, in_=big[c].v, func=AF.Square)
            else:
                cx.op("dve", "tensor_tensor", out=sq.v, in0=big[c].v, in1=big[c].v, op=ALU.mult)
            ps = psum()
            cx.mm(ps.v, [(onesb, sq.v)])
            return ps

        def l2_b(c, ps):
            ta, tb_ = (tf[0], tf[1]) if c % 2 == 0 else (tf[2], tf[3])
            cx.op("act", "activation", out=ta.v, in_=ps.v, func=AF.Ln, bias=EPS)
            cx.op("act", "activation", out=tb_.v, in_=ta.v, func=AF.Exp, scale=-0.5,
                  bias=(float(np.log(128.0 ** -0.5)) if c < 8 else 0.0))
            cx.op("dve", "tensor_tensor", out=big[c].v, in0=big[c].v, in1=tb_.v, op=ALU.mult)
        pend = None
        for c in range(16):
            ps = l2_a(c)
            if pend is not None:
                l2_b(*pend)
            pend = (c, ps)
        l2_b(*pend)
        sl_ba = take()
        gdn_all(l, sl_ba)
        def on_a(hd):
            sq = tb[0] if hd % 2 == 0 else tb[1]
            if hd % 2 == 0:
                cx.op("act", "activation", out=sq.v, in_=b8b[hd].v, func=AF.Square)
            else:
                cx.op("dve", "tensor_tensor", out=sq.v, in0=b8b[hd].v, in1=b8b[hd].v, op=ALU.mult)
            ps = psum()
            cx.mm(ps.v, [(onesb, sq.v)])
            return ps

        def on_b(hd, ps):
            ta, tb_, tc = (tf[0], tf[1], tf[4]) if hd % 2 == 0 else (tf[2], tf[3], tf[5])
            cx.op("act", "activation", out=ta.v, in_=ps.v, func=AF.Ln, scale=1.0 / 128.0, bias=EPS)
            cx.op("act", "activation", out=tb_.v, in_=ta.v, func=AF.Exp, scale=-0.5)
            cx.op("dve", "tensor_tensor", out=tc.v, in0=b8b[hd].v, in1=tb_.v, op=ALU.mult)
            cx.op("dve", "scalar_tensor_tensor", out=b8b[hd].v, in0=tc.v, scalar=vcol(V_ONW + l),
                  in1=b8a[hd].v, op0=ALU.mult, op1=ALU.mult)
        pend = None
        for hd in range(8):
            ps = on_a(hd)
            if pend is not None:
                on_b(*pend)
            pend = (hd, ps)
        on_b(*pend)
        for oc in range(8):
            sl = take()
            ps = psum()
            cx.mm(ps.v, [(wt(sl, kc), b8b[kc].v) for kc in range(8)])
            cx.op("dve", "tensor_tensor", out=h[oc].v, in0=ps.v, in1=h[oc].v, op=ALU.add)

    def gdn_gates(l, sl_ba, G):
        g_eb, g_beta, g_xa, g_ax, g_la, g_g, g_e3 = G
        psg = psum()
        ins = []
        for blk in range(4):
            for kc in range(8):
                ins.append((psg[:, blk * 16:(blk + 1) * 16], xn[kc][:, blk * 128:(blk + 1) * 128], sl_ba[:, kc * 16:(kc + 1) * 16],
                            kc == 0, kc == 7, False))
        cx.pe([psg], ins)
        psg3 = View(psg, psg.ap[:, 0:64].rearrange("p (b e) -> p b e", e=16))

        def v3(tile):
            return View(tile, tile.ap[:, 0:32].rearrange("p (b e) -> p b e", e=8))
        cx.op("act", "activation", out=v3(g_eb), in_=psg3[:, :, 0:8], func=AF.Exp, scale=-1.0)
        cx.op("dve", "tensor_scalar", out=g_eb[:, 0:32], in0=g_eb[:, 0:32], scalar1=1.0, scalar2=None, op0=ALU.add)
        cx.op("dve", "reciprocal", out=g_beta[:, 0:32], in_=g_eb[:, 0:32])
        cx.op("dve", "tensor_tensor", out=v3(g_xa), in0=psg3[:, :, 8:16],
              in1=_bc(vecs[:, V_DTB + l * 8:V_DTB + l * 8 + 8], [128, 4, 8], 1), op=ALU.add)
        cx.op("dve", "scalar_tensor_tensor", out=g_ax[:, 0:32], in0=g_xa[:, 0:32], scalar=-1.0, in1=g_xa[:, 0:32],
              op0=ALU.mult, op1=ALU.max)
        cx.op("act", "activation", out=g_la[:, 0:32], in_=g_ax[:, 0:32], func=AF.Exp, scale=-1.0)
        cx.op("act", "activation", out=g_la[:, 0:32], in_=g_la[:, 0:32], func=AF.Ln, bias=1.0)
        cx.op("dve", "scalar_tensor_tensor", out=g_g[:, 0:32], in0=g_xa[:, 0:32], scalar=0.0, in1=g_la[:, 0:32],
              op0=ALU.max, op1=ALU.add)
        cx.op("dve", "tensor_tensor", out=v3(g_g), in0=v3(g_g), in1=_bc(negA[:, l * 8:l * 8 + 8], [128, 4, 8], 1), op=ALU.mult)
        psc = psum()
        cx.pe([psc], [(psc[:, 0:32], Umask, g_g[:, 0:32], True, True, False),
                      (psc[:, 32:64], SLo, g_g[:, 0:32], True, True, False),
                      (psc[:, 64:96], ones, g_g[:, 0:32], True, True, False)])
        cx.op("act", "activation", out=g_e3[:, 0:96], in_=psc[:, 0:96], func=AF.Exp)

    def gdn_unit(l, blk, hg, tok, G, TS):
        g_eb, g_beta, g_xa, g_ax, g_la, g_g, g_e3 = G
        tbs, tfs = TS

        def D(name, view):
            if l == DUMP_L and blk < 2 and st.get("ti", 0) == 0 and hg == DUMP_HG:
                dump("%s_b%d" % (name, blk), view)

        def hv(bufv, i):
            return bufv[:, i * 128:(i + 1) * 128]

        def bcast_gate(tile, col0):
            c0 = (col0 // 8) * 32 + blk * 8 + hg * 4
            return _bc(tile[:, c0:c0 + 4], [128, 4, 128], 2)

        def as3(v):
            return View(v.buf, v.ap.rearrange("p (a b) -> p a b", a=4))

        if hg == DUMP_HG:
            D("beta", g_beta[:, blk * 8:blk * 8 + 8]); D("g", g_g[:, blk * 8:blk * 8 + 8])
        my_psb = psb[3 * hg:3 * hg + 3]
        cnt = {"n": 0}

        def psum():
            cnt["n"] += 1
            return my_psb[cnt["n"] % 3]

        def psumT():
            return pst[hg]
        kT_v = View(big[8 + hg * 4], big_all[:, 8 + hg * 4:12 + hg * 4, tok])
        qT_v = View(big[hg * 4], big_all[:, hg * 4:hg * 4 + 4, tok])
        qbufs = [big[hg * 4 + i] for i in range(4)]
        (K_eg, K_e, Vt, DTc, DTs, N2, attnT, M2, PA, QA, RA, PB, QB, RB, W2n, QdT, vnew) = tbs
        GU, egrow = tfs
        Ex = GU
        pk = psumT()
        cx.pe([pk], [(hv(pk.v, i), big[8 + hg * 4 + i][:, tok], identb, True, True, True) for i in range(4)])
        cx.op("pool", "tensor_tensor", out=as3(GU.v), in0=_bc(Umask, [128, 4, 128], 1), in1=bcast_gate(g_g, 0), op=ALU.mult)
        pD = psum()
        cx.mm(pD.v, [(SLo, GU.v)])
        pR = psum()
        cx.mm(pR.v, [(ones, GU.v)])
        yield
        cx.op("dve", "tensor_tensor", out=as3(K_eg.v), in0=as3(pk[:, 0:512]), in1=bcast_gate(g_e3, 0), op=ALU.mult)
        cx.op("dve", "tensor_tensor", out=as3(K_e.v), in0=as3(pk[:, 0:512]), in1=bcast_gate(g_e3, 8), op=ALU.mult)
        cx.op("act", "activation", out=Ex.v, in_=pD.v, func=AF.Exp)
        cx.op("act", "activation", out=egrow.v, in_=pR.v, func=AF.Exp)
        cx.op("pool", "tensor_tensor", out=as3(DTs.v), in0=as3(Ex.v), in1=_bc(SU, [128, 4, 128], 1), op=ALU.mult)
        cx.op("pool", "tensor_tensor", out=as3(DTc.v), in0=as3(Ex.v), in1=_bc(Umask, [128, 4, 128], 1), op=ALU.mult)
        cx.op("pool", "tensor_tensor", out=as3(DTs.v), in0=as3(DTs.v), in1=bcast_gate(g_beta, 0), op=ALU.mult)
        pv = psumT()
        cx.pe([pv], [(hv(pv.v, i), big[16 + hg * 4 + i][:, tok], identb, True, True, True) for i in range(4)])
        pA1 = psum()
        cx.pe([pA1], [(hv(pA1.v, i), big[8 + hg * 4 + i][:, tok], big[8 + hg * 4 + i][:, tok], True, True, False) for i in range(4)])
        pA2 = psum()
        cx.pe([pA2], [(hv(pA2.v, i), big[8 + hg * 4 + i][:, tok], big[hg * 4 + i][:, tok], True, True, False) for i in range(4)])
        yield
        cx.op("act", "activation", out=Vt.v, in_=pv[:, 0:512], func=AF.Copy)
        cx.op("dve", "tensor_tensor", out=N2.v, in0=pA1.v, in1=DTs.v, op=ALU.mult)
        cx.op("dve", "tensor_tensor", out=attnT.v, in0=pA2.v, in1=DTc.v, op=ALU.mult)
        D("K_eg", K_eg.v); D("K_e", K_e.v); D("Vt", Vt.v); D("Ex", Ex.v); D("egrow", egrow.v)
        D("DTc", DTc.v); D("N2", N2.v); D("attnT", attnT.v)
        pm = psumT()
        cx.pe([pm], [(hv(pm.v, i), hv(N2.v, i), identb, True, True, True) for i in range(4)])
        yield
        cx.op("act", "activation", out=M2.v, in_=pm[:, 0:512], func=AF.Copy)
        D("M2", M2.v)
        cx.op("pool", "scalar_tensor_tensor" if False else "tensor_tensor", out=as3(RA.v), in0=_bc(identb, [128, 4, 128], 1), in1=as3(N2.v), op=ALU.subtract)
        p2 = psum()
        cx.pe([p2], [(hv(p2.v, i), hv(M2.v, i), hv(N2.v, i), True, True, False) for i in range(4)])
        pq = psum()
        cx.pe([pq], [(hv(pq.v, i), hv(N2.v, i), hv(M2.v, i), True, True, False) for i in range(4)])
        yield
        P, Q, R = PB, QB, RA
        cx.op("dve", "tensor_copy", out=P.v, in_=p2.v)
        cx.op("act", "activation", out=Q.v, in_=pq.v, func=AF.Copy)
        sets = [(PA, QA, RB), (PB, QB, RA)]
        for k in range(1, 7):
            Pn, Qn, Rn = sets[(k - 1) % 2]
            pr = psum()
            cx.pe([pr], [(hv(pr.v, i), hv(Q.v, i), hv(R.v, i), True, True, False) for i in range(4)])
            if k < 6:
                p2 = psum()
                cx.pe([p2], [(hv(p2.v, i), hv(Q.v, i), hv(P.v, i), True, True, False) for i in range(4)])
                pq = psum()
                cx.pe([pq], [(hv(pq.v, i), hv(P.v, i), hv(Q.v, i), True, True, False) for i in range(4)])
            yield
            cx.op("dve", "tensor_tensor", out=Rn.v, in0=pr.v, in1=R.v, op=ALU.add)
            if k < 6:
                cx.op("act", "activation", out=Qn.v, in_=pq.v, func=AF.Copy)
                cx.op("dve", "tensor_copy", out=Pn.v, in_=p2.v)
            P, Q, R = Pn, Qn, Rn
        spare = [PA, QA, PB, QB]
        for it in range(N_NEWTON):
            XT, En = spare[0], spare[1]
            Xn = spare[2 + it % 2]
            pX = psumT()
            cx.pe([pX], [(hv(pX.v, i), hv(R.v, i), identb, True, True, True) for i in range(4)])
            pE = psum()
            ins = []
            for i in range(4):
                ins.append((hv(pE.v, i), hv(M2.v, i), hv(R.v, i), True, False, False))
                ins.append((hv(pE.v, i), identb, hv(R.v, i), False, True, False))
            cx.pe([pE], ins)
            yield
            cx.op("act", "activation", out=XT.v, in_=pX[:, 0:512], func=AF.Copy)
            cx.op("dve", "scalar_tensor_tensor", out=as3(En.v), in0=as3(pE.v), scalar=-1.0,
                  in1=_bc(identb, [128, 4, 128], 1), op0=ALU.mult, op1=ALU.add)
            pN = psum()
            cx.pe([pN], [(hv(pN.v, i), hv(XT.v, i), hv(En.v, i), True, True, False) for i in range(4)])
            yield
            cx.op("dve", "tensor_tensor", out=Xn.v, in0=pN.v, in1=R.v, op=ALU.add)
            R = Xn
        T2T = R
        D("T2T", T2T.v)
        pw = psum()
        cx.pe([pw], [(hv(pw.v, i), hv(K_eg.v, i), hv(T2T.v, i), True, True, False) for i in range(4)])
        cx.op("dve", "tensor_tensor", out=as3(QdT.v), in0=qT_v, in1=as3(egrow.v), op=ALU.mult, _reads=qbufs)
        yield
        cx.op("act", "activation", out=W2n.v, in_=pw.v, func=AF.Copy, scale=-1.0)
        D("W2n", W2n.v); D("QdT", QdT.v)
        Sb = Sbf[l][hg]
        S3 = S32[l][hg]
        pV = psum()
        ins = []
        for i in range(4):
            ins.append((hv(pV.v, i), hv(T2T.v, i), hv(Vt.v, i), True, False, False))
            ins.append((hv(pV.v, i), hv(W2n.v, i), hv(Sb.v, i), False, True, False))
        cx.pe([pV], ins)
        yield
        cx.op("dve", "tensor_tensor", out=as3(vnew.v), in0=as3(pV.v), in1=bcast_gate(g_beta, 0), op=ALU.mult)
        D("vnew", vnew.v)
        pO = psum()
        ins = []
        for i in range(4):
            ins.append((hv(pO.v, i), hv(Sb.v, i), hv(QdT.v, i), True, False, False))
            ins.append((hv(pO.v, i), hv(vnew.v, i), hv(attnT.v, i), False, True, False))
        cx.pe([pO], ins)
        pS = psum()
        cx.pe([pS], [(hv(pS.v, i), hv(K_e.v, i), hv(vnew.v, i), True, True, False) for i in range(4)])
        cx.op("pool", "tensor_tensor", out=as3(S3.v), in0=as3(S3.v), in1=bcast_gate(g_e3, 16), op=ALU.mult)
        yield
        obufs = [b8b[hg * 4 + i] for i in range(4)]
        cx.op("act", "activation", out=View(obufs[0], bb_all[:, hg * 4:hg * 4 + 4, tok]), in_=as3(pO.v),
              func=AF.Copy, _writes=obufs[1:])
        cx.op("dve", "tensor_tensor", out=S3.v, in0=pS.v, in1=S3.v, op=ALU.add)
        cx.op("act", "activation", out=Sb.v, in_=S3.v, func=AF.Copy)
        D("S3", S3.v); D("oT", View(obufs[0], bb_all[:, hg * 4:hg * 4 + 4, tok])); D("qT", qT_v); D("kT", kT_v)

    def gdn_all(l, sl_ba):
        toks = [slice(b_ * 128, (b_ + 1) * 128) for b_ in range(4)]
        gdn_gates(l, sl_ba, gsm)

        def stream(hg):
            for blk in range(4):
                yield from gdn_unit(l, blk, hg, toks[blk], gsm, TSETS[hg])
        gens = [stream(0), stream(1)]
        while gens:
            for g in list(gens):
                try:
                    next(g)
                except StopIteration:
                    gens.remove(g)

    def kv_phase(ti):
        par = ti % 2
        rmsnorm(N_KVNORM, xn)
        for oc in range(8):
            sl = take()
            ps = proj(sl, xn)
            cx.op("act", "activation", out=KT[oc][:, par * T:(par + 1) * T], in_=ps.v, func=AF.Copy)
        held = [take() for _ in range(4)]
        for cg in range(2):
            if cg == 1:
                held = [take() for _ in range(4)]
            for blk in range(4):
                ps = psum()
                pairs = []
                for kc in range(8):
                    idx = kc
                    pairs.append((xn[kc][:, blk * 128:(blk + 1) * 128], held[idx // 2][:, (idx % 2) * 512:(idx % 2) * 512 + 512]))
                cx.mm(ps.v, pairs)
                vout = View(VT[par * 4 + blk], vt_all[:, par * 4 + blk, :].rearrange("p (h e) -> p h e", e=65)[:, cg * 8:(cg + 1) * 8, 0:64])
                vin = View(ps, ps.ap.rearrange("p (h e) -> p h e", e=64))
                if blk % 2 == 0:
                    cx.op("act", "activation", out=vout, in_=vin, func=AF.Copy)
                else:
                    cx.op("dve", "tensor_copy", out=vout, in_=vin)

    def b_layer(j, ti):
        par = ti % 2
        rmsnorm(N_BNORM + j, xn)
        for oc in range(8):
            sl = take()
            ps = proj(sl, xn)
            qa, qb = big[8 + 2 * oc], big[9 + 2 * oc]
            cx.op("pool", "memset", ap=qa[64:128, :], constant=0.0)
            cx.op("pool", "memset", ap=qb[0:64, :], constant=0.0)
            cx.op("act", "activation", out=qa[0:64, :], in_=ps[0:64, :], func=AF.Copy)
            cx.op("dve", "tensor_copy", out=qb[64:128, :], in_=ps[64:128, :])
        eb = {}

        def att_a(hg, m):
            it_ = hg * 4 + m
            for kb in range(5):
                si = (hg * 5 + kb) // 2
                while si not in eb:
                    eb[len(eb)] = take()
            qs = slice(m * 128, (m + 1) * 128)
            kbs = [kb for kb in range(5) if not (ti == 0 and m + kb - 4 < 0)]
            PT = {}
            for n_, kb in enumerate(kbs):
                rel = m + kb - 4
                if rel < 0:
                    kpar, kblk = 1 - par, 4 + rel
                else:
                    kpar, kblk = par, rel
                kcols = slice(kpar * T + kblk * 128, kpar * T + kblk * 128 + 128)
                pS = psum()
                ins = []
                for hh in range(4):
                    head = hg * 4 + hh
                    ins.append((pS[:, hh * 128:(hh + 1) * 128], KT[head // 2][:, kcols], big[8 + head][:, qs], True, True, False))
                cx.pe([pS], ins)
                ex = tf[2 + (it_ * 5 + n_) % 4]
                cx.op("act", "activation", out=ex.v, in_=pS.v, func=AF.Exp, scale=0.125)
                idx = hg * 5 + kb
                ebv = eb[idx // 2][:, (idx % 2) * 512:(idx % 2) * 512 + 512]
                pt = tb[(it_ % 2) * 5 + n_]
                cx.op("pool" if n_ % 2 == 1 else "dve", "tensor_tensor", out=pt.v, in0=ex.v, in1=ebv, op=ALU.mult)
                PT[kb] = (pt, kpar * 4 + kblk)
            return (hg, m, kbs, PT)

        def att_b(hg, m, kbs, PT):
            pO = psb[6 + (hg * 4 + m) % 2] if len(psb) > 6 else psum()
            ins = []
            for hh in range(4):
                head = hg * 4 + hh
                for n_, kb in enumerate(kbs):
                    pt, vb = PT[kb]
                    ins.append((pO[:, hh * 65:(hh + 1) * 65], pt[:, hh * 128:(hh + 1) * 128],
                                VT[vb][:, head * 65:(head + 1) * 65], n_ == 0, n_ == len(kbs) - 1, False))
            cx.pe([pO], ins)
            pO3 = View(pO, pO.ap[:, 0:260].rearrange("p (h e) -> p h e", e=65))
            rd = rdt
            cx.op("dve", "reciprocal", out=rd[:, 0:4], in_=pO3[:, :, 64])
            otm = View(big[2 * m], big_all[:, 2 * m:2 * m + 2, :].rearrange("p a b -> p (a b)")[:, hg * 256:(hg + 1) * 256]
                       .rearrange("p (h e) -> p h e", e=64))
            cx.op("dve", "tensor_tensor", out=otm, in0=pO3[:, :, 0:64], in1=_bc(rd[:, 0:4], [128, 4, 64], 2),
                  op=ALU.mult, _writes=[big[2 * m + 1]])
        pend = None
        for hg in range(4):
            for m in range(4):
                cur = att_a(hg, m)
                if pend is not None:
                    att_b(*pend)
                pend = cur
        att_b(*pend)
        for m in range(4):
            qs = slice(m * 128, (m + 1) * 128)
            otm2 = big_all[:, 2 * m:2 * m + 2, :].rearrange("p a b -> p (a b)")
            pT = psumT()
            cx.pe([pT], [(pT[:, c * 128:(c + 1) * 128], View(big[2 * m + c // 4], otm2[:, c * 128:(c + 1) * 128]), identb, True, True, True)
                         for c in range(8)])
            cx.op("act" if m % 2 == 0 else "dve", "activation" if m % 2 == 0 else "tensor_copy",
                  out=View(b8b[0], bb_all[:, 0:8, qs]), in_=View(pT, pT.ap.rearrange("p (a b) -> p a b", a=8)),
                  _writes=b8b[1:], **({"func": AF.Copy} if m % 2 == 0 else {}))
        while len(eb) < 10:
            eb[len(eb)] = take()
        for oc in range(8):
            sl = take()
            ps = psum()
            cx.mm(ps.v, [(wt(sl, kc), b8b[kc].v) for kc in range(8)])
            cx.op("dve", "tensor_tensor", out=h[oc].v, in0=ps.v, in1=h[oc].v, op=ALU.add)

    d_in = [cx.dma_sem("in%d" % i) for i in range(2)]
    d_out = [cx.dma_sem("out%d" % i) for i in range(2)]
    d_dbg = [cx.dma_sem("dbg%d" % i) for i in range(2)]
    io_n = {"i": 0}

    d_pre = [cx.dma_sem("pre%d" % i) for i in range(4)]
    pre_state = {"tile": -1}

    def prefetch_x(ti):
        if ti >= NT:
            return
        for blk in range(2):
            r0 = ti * T + blk * 128
            for half in range(2):
                cx.dma("sync", tf[2 + blk * 2 + half].v, x_d[r0:r0 + 128, half * 512:(half + 1) * 512], d_pre[blk * 2 + half])
        pre_state["tile"] = ti

    def load_tile(ti):
        for blk in range(4):
            pre = (pre_state["tile"] == ti and blk < 2)
            if not pre:
                k = io_n["i"] % 2
                io_n["i"] += 1
                iob = io32[k]
                r0 = ti * T + blk * 128
                cx.dma("sync", iob.v, x_d[r0:r0 + 128, :], d_in[k])
            for half in range(2):
                ps = psum()
                if pre:
                    src = tf[2 + blk * 2 + half]
                    cx.pe([ps], [(ps[:, i * 128:(i + 1) * 128], src[:, i * 128:(i + 1) * 128], ident, True, True, True)
                                 for i in range(4)])
                else:
                    cx.pe([ps], [(ps[:, i * 128:(i + 1) * 128], iob[:, (half * 4 + i) * 128:(half * 4 + i + 1) * 128], ident, True, True, True)
                                 for i in range(4)])
                hb = [h[half * 4 + i] for i in range(4)]
                cx.op("act" if half == 0 else "dve", "activation" if half == 0 else "tensor_copy",
                      out=View(hb[0], h_all[:, half * 4:half * 4 + 4, blk * 128:(blk + 1) * 128]),
                      in_=View(ps, ps.ap.rearrange("p (a b) -> p a b", a=4)), _writes=hb[1:],
                      **({"func": AF.Copy} if half == 0 else {}))

    def store_tile(src_chunks, dst_d, ti, sems):
        for blk in range(4):
            k = io_n["i"] % 2
            io_n["i"] += 1
            iob = io32[k]
            for half in range(2):
                ps = psum()
                cx.pe([ps], [(ps[:, i * 128:(i + 1) * 128], src_chunks[half * 4 + i][:, blk * 128:(blk + 1) * 128], ident, True, True, True)
                             for i in range(4)])
                if half == 0:
                    cx.op("act", "activation", out=iob[:, 0:512], in_=ps.v, func=AF.Copy)
                else:
                    cx.op("dve", "tensor_copy", out=iob[:, 512:1024], in_=ps.v)
            r0 = ti * T + blk * 128
            cx.dma("sync", dst_d[r0:r0 + 128, :], iob.v, sems[k])

    def dbg_dump(tag, ti):
        if tag in debug_h:
            store_tile(h, dbg_d[debug_h.index(tag)], ti, d_dbg)

    def skip(n):
        for _ in range(n):
            take()

    for ti in range(NT):
        if ti > 0:
            cx.new_epoch()
        st["ti"] = ti
        load_tile(ti)
        for l in range(2):
            if ("a%d" % l) in phases:
                a_layer(l, ti)
            else:
                skip(N_A)
            dbg_dump("a%d" % l, ti)
            if ("f%d" % l) in phases:
                ffn(l)
            else:
                skip(N_F)
            dbg_dump("f%d" % l, ti)
        if "kv" in phases:
            kv_phase(ti)
        else:
            skip(N_KV)
        for j in range(2):
            if ("b%d" % j) in phases:
                b_layer(j, ti)
            else:
                skip(N_B)
            dbg_dump("b%d" % j, ti)
            if j == 1:
                prefetch_x(ti + 1)
            if ("f%d" % (2 + j)) in phases:
                ffn(2 + j)
            else:
                skip(N_F)
            dbg_dump("f%d" % (2 + j), ti)
        if "fin" in phases:
            rmsnorm(N_FINAL, None, f32_dst=h)
        store_tile(h, y_d, ti, d_out)
    cx.finish(d_out + d_dbg + [d_dump])
    assert rs["use"] == total_uses, (rs["use"], total_uses)
    return nc, cx


def _proj_slots(W, ocs):
    out = []
    for oc in ocs:
        blk = W[:, oc * 128:(oc + 1) * 128].reshape(8, 128, 128)
        out.append(blk.transpose(1, 0, 2).reshape(128, 1024))
    return out


def pack_wstream(inp):
    f32 = np.float32
    slots = []

    def ffn(l):
        Wup = inp["f_w_up"][l]
        order = [c for cc in range(22) for c in (cc, 22 + cc)]
        slots.extend(_proj_slots(Wup, order))
        Wd = inp["f_w_down"][l]
        tiles = Wd.reshape(22, 128, 8, 128).transpose(2, 0, 1, 3).reshape(176, 128, 128)
        sl = tiles.reshape(22, 8, 128, 128).transpose(0, 2, 1, 3).reshape(22, 128, 1024)
        slots.extend(list(sl))

    for l in range(2):
        Win = inp["a_w_in"][l]
        slots.extend(_proj_slots(Win, range(32)))
        ba = np.zeros((128, 1024), f32)
        for kc in range(8):
            ba[:, kc * 16:(kc + 1) * 16] = Win[kc * 128:(kc + 1) * 128, 4096:4112]
        slots.append(ba)
        slots.extend(_proj_slots(inp["a_w_out"][l], range(8)))
        ffn(l)
    wkv = inp["w_kv"]
    slots.extend(_proj_slots(wkv, range(8)))
    vs = np.zeros((8, 128, 1024), f32)
    for idx in range(16):
        cg, kc = divmod(idx, 8)
        vs[idx // 2][:, (idx % 2) * 512:(idx % 2) * 512 + 512] = wkv[kc * 128:(kc + 1) * 128, 1024 + cg * 512:1024 + (cg + 1) * 512]
    slots.extend(list(vs))
    for j in range(2):
        slots.extend(_proj_slots(inp["b_w_q"][j], range(8)))
        slots.extend([np.zeros((128, 1024), f32)] * 10)
        slots.extend(_proj_slots(inp["b_w_out"][j], range(8)))
        ffn(2 + j)
    assert len(slots) == NSTREAM, len(slots)
    return np.ascontiguousarray(np.stack(slots, 0).reshape(NSTREAM * 128, 1024).astype(f32))


def pack_relb(rel_bias):
    out = np.zeros((2, 128, 80 * 128), np.float32)
    k = np.arange(128)[:, None]
    q = np.arange(128)[None, :]
    for j in range(2):
        for hg in range(4):
            for kb in range(5):
                rel = np.clip(q + 512 - k - 128 * kb, -256, 256) + 256
                for hh in range(4):
                    tile = rel_bias[j, hg * 4 + hh][rel].astype(np.float32)
                    if kb == 0:
                        tile[:64, 64:] = -30000.0
                    if kb == 4:
                        tile[64:, :64] = -30000.0
                    t_idx = (hg * 5 + kb) * 4 + hh
                    out[j, :, t_idx * 128:(t_idx + 1) * 128] = tile
    return out


def pack_vecs(inp):
    v = np.zeros((128, NV), np.float32)

    def ch(x):
        return np.asarray(x, np.float32).reshape(-1, 128).T
    norms = [inp["a_norm"][0], inp["a_norm"][1], inp["kv_norm"], inp["b_norm"][0], inp["b_norm"][1],
             inp["f_norm"][0], inp["f_norm"][1], inp["f_norm"][2], inp["f_norm"][3], inp["final_norm"]]
    for i, g in enumerate(norms):
        v[:, V_NORM + i * 8:V_NORM + i * 8 + 8] = ch(g)
    for l in range(2):
        w = inp["a_conv"][l]
        blk = np.stack([ch(w[k]) for k in range(4)], -1)
        v[:, V_ACONV + l * 96:V_ACONV + (l + 1) * 96] = blk.reshape(128, 96)
        v[:, V_ONW + l] = inp["a_out_norm"][l]
        v[:, V_ALOG + l * 8:V_ALOG + l * 8 + 8] = inp["a_A_log"][l][None, :]
        v[:, V_DTB + l * 8:V_DTB + l * 8 + 8] = inp["a_dt_bias"][l][None, :]
    for l in range(4):
        w = inp["f_conv"][l]
        blk = np.stack([ch(w[k]) for k in range(3)], -1)
        v[:, V_FCONV + l * 132:V_FCONV + (l + 1) * 132] = blk.reshape(128, 132)
        v[:, V_FBIAS + l * 44:V_FBIAS + (l + 1) * 44] = ch(inp["f_conv_b"][l])
    return v


def pack_cst():
    p = np.arange(128)[:, None]
    f = np.arange(128)[None, :]
    c = np.zeros((128, 5 * 128), np.float32)
    c[:, C_ID:C_ID + 128] = (p == f)
    c[:, C_U:C_U + 128] = (p <= f)
    c[:, C_SLO:C_SLO + 128] = (p > f)
    c[:, C_SU:C_SU + 128] = (f > p)
    c[:, C_ONE:C_ONE + 128] = 1.0
    return c


N_CORES = 8


def kernel(**inputs):
    inp = {k: np.asarray(v) for k, v in inputs.items()}
    x = inp["x"]
    B = x.shape[0]
    nc, _ = build_program(NT_FULL)
    wst = pack_wstream(inp)
    relb = pack_relb(inp["b_rel_bias"])
    vecs = pack_vecs(inp)
    cst = pack_cst()
    in_maps = []
    for c in range(N_CORES):
        in_maps.append({"x": np.ascontiguousarray(x[c % B]), "wst": wst, "relb": relb, "vecs": vecs, "cst": cst})
    res = run_bass_kernel_spmd(nc, in_maps, core_ids=list(range(N_CORES)))
    out = np.stack([np.asarray(res.results[b]["y"]) for b in range(B)], 0)
    return out.astype(np.float32)
```

```python
import numpy as np
import concourse.bass as bass
import concourse.mybir as mybir
from concourse.bass_utils import run_bass_kernel_spmd

F32 = mybir.dt.float32
BF16 = mybir.dt.bfloat16
AF = mybir.ActivationFunctionType
ALU = mybir.AluOpType


class Buf:
    __slots__ = ("name", "ap", "w", "r")

    def __init__(self, name, ap):
        self.name = name
        self.ap = ap
        self.w = None
        self.r = {}

    def __getitem__(self, idx):
        return View(self, self.ap[idx])

    @property
    def v(self):
        return View(self, self.ap)


class View:
    __slots__ = ("buf", "ap")

    def __init__(self, buf, ap):
        self.buf = buf
        self.ap = ap

    def __getitem__(self, idx):
        return View(self.buf, self.ap[idx])


class DSem:
    def __init__(self, sem):
        self.sem = sem
        self.count = 0


class Ctx:
    def __init__(self, nc, same_engine_sync=False):
        self.nc = nc
        self.eng = {"pe": nc.tensor, "act": nc.scalar, "dve": nc.vector, "pool": nc.gpsimd, "sync": nc.sync}
        self.sem = {}
        self.waited = {}
        self.same_engine_sync = same_engine_sync
        self.small_thresh = 512
        self.n_ins = 0
        self.n_wait = 0

    def new_epoch(self):
        self.n_epoch = getattr(self, "n_epoch", 0) + 1
        for e in ("pe", "act", "dve", "pool"):
            self.sem[e] = [self.nc.alloc_semaphore(name="s_%s_%d" % (e, self.n_epoch)), 0]

    def dma_sem(self, name=None):
        self.n_dsem = getattr(self, "n_dsem", 0) + 1
        return DSem(self.nc.alloc_semaphore(name="d_%s_%d" % (name, self.n_dsem)))

    def sb(self, name, shape, dtype):
        t = self.nc.alloc_sbuf_tensor(name, shape, dtype)
        return Buf(name, t.ap())

    def ps(self, name, shape, dtype=F32):
        t = self.nc.alloc_psum_tensor(name, shape, dtype)
        return Buf(name, t.ap())

    def wrap(self, name, ap):
        return Buf(name, ap)

    def _wait(self, engname, deps):
        best = {}
        for d in deps:
            if d is None:
                continue
            sem, cnt, src = d[0], d[1], d[2]
            if src == engname and not self.same_engine_sync and not (len(d) > 3 and d[3]):
                continue
            k = id(sem)
            if k not in best or best[k][1] < cnt:
                best[k] = (sem, cnt)
        e = self.eng[engname]
        for k, (sem, cnt) in best.items():
            if self.waited.get((engname, k), 0) >= cnt:
                continue
            e.wait_ge(sem, cnt)
            self.n_wait += 1
            self.waited[(engname, k)] = cnt

    def _deps(self, reads, writes):
        deps = []
        for b in reads:
            deps.append(b.w)
        for b in writes:
            deps.append(b.w)
            deps.extend(b.r.values())
        return deps

    def _record(self, ev, reads, writes):
        k = id(ev[0])
        for b in reads:
            b.r[k] = ev
        for b in writes:
            b.w = ev
            b.r = {}

    def op(self, engname, method, _reads=(), _writes=(), **kw):
        reads, writes = list(_reads), list(_writes)
        args = {}
        for k, v in kw.items():
            if isinstance(v, View):
                (writes if k in ("out", "accum_out", "ap") else reads).append(v.buf)
                args[k] = v.ap
            else:
                args[k] = v
        self._wait(engname, self._deps(reads, writes))
        ins = getattr(self.eng[engname], method)(**args)
        s = self.sem[engname]
        s[1] += 1
        ins.then_inc(s[0], 1)
        self.n_ins += 1
        small = True
        o = kw.get("out", kw.get("ap"))
        if isinstance(o, View):
            shp = o.ap.shape
            n = 1
            for d_ in shp[1:]:
                n *= int(d_)
            small = n < self.small_thresh
        self._record((s[0], s[1], engname, small), reads, writes)
        return ins

    def pe(self, outbufs, instrs, _reads=()):
        reads = list(_reads)
        writes = list(outbufs)
        for o, l, r, st, sp, tr in instrs:
            reads.append(l.buf)
            reads.append(r.buf)
            if o.buf not in writes:
                writes.append(o.buf)
        self._wait("pe", self._deps(list(_reads), writes))
        pe = self.eng["pe"]
        ins = None
        for o, l, r, st, sp, tr in instrs:
            self._wait("pe", [l.buf.w, r.buf.w])
            if tr:
                ins = pe.transpose(o.ap, l.ap, r.ap)
            else:
                ins = pe.matmul(o.ap, l.ap, r.ap, start=st, stop=sp)
            self.n_ins += 1
        s = self.sem["pe"]
        s[1] += 1
        ins.then_inc(s[0], 1)
        self._record((s[0], s[1], "pe"), reads, writes)

    def mm(self, out, pairs):
        n = len(pairs)
        self.pe([out.buf], [(out, l, r, i == 0, i == n - 1, False) for i, (l, r) in enumerate(pairs)])

    def dma(self, qname, out, in_, dsem):
        reads = [in_.buf] if isinstance(in_, View) else []
        writes = [out.buf] if isinstance(out, View) else []
        self._wait(qname, self._deps(reads, writes))
        o = out.ap if isinstance(out, View) else out
        i = in_.ap if isinstance(in_, View) else in_
        ins = self.eng[qname].dma_start(out=o, in_=i)
        dsem.count += 16
        ins.then_inc(dsem.sem, 16)
        self.n_ins += 1
        self._record((dsem.sem, dsem.count, "dma"), reads, writes)

    def finish(self, dsems):
        for d in dsems:
            self.eng["sync"].wait_ge(d.sem, d.count)


T = 512
NT_FULL = 16
EPS = 1e-6
NSLOT = 9
HOLD = 5

N_A, N_F, N_KV, N_B = 41, 66, 16, 26
NSTREAM = 2 * N_A + 4 * N_F + N_KV + 2 * N_B
EB_SLOT = [2 * (N_A + N_F) + N_KV + 8, 2 * (N_A + N_F) + N_KV + N_B + N_F + 8]

V_NORM = 0
V_ACONV = 80
V_FCONV = V_ACONV + 2 * 96
V_FBIAS = V_FCONV + 4 * 132
V_ONW = V_FBIAS + 4 * 44
V_ALOG = V_ONW + 2
V_DTB = V_ALOG + 16
NV = V_DTB + 16
N_ANORM, N_KVNORM, N_BNORM, N_FNORM, N_FINAL = 0, 2, 3, 5, 9

C_ID, C_U, C_SLO, C_SU, C_ONE = 0, 128, 256, 384, 512


def _bc(view, shape, axis):
    return View(view.buf, view.ap.unsqueeze(axis).broadcast_to(list(shape)))


N_NEWTON = 1
DUMP_L = 0
DUMP_HG = 0
ALL_PHASES = ("a0", "f0", "a1", "f1", "kv", "b0", "f2", "b1", "f3", "fin")


def build_program(NT=NT_FULL, debug_h=(), phases=ALL_PHASES, dumps=False):
    nc = bass.Bass("TRN2", target_bir_lowering=False)
    cx = Ctx(nc)
    S = NT * T
    x_d = nc.dram_tensor("x", [S, 1024], F32, kind="ExternalInput").ap()
    wst_d = nc.dram_tensor("wst", [NSTREAM * 128, 1024], F32, kind="ExternalInput").ap()
    relb_d = nc.dram_tensor("relb", [2, 128, 80 * 128], F32, kind="ExternalInput").ap()
    vecs_d = nc.dram_tensor("vecs", [128, NV], F32, kind="ExternalInput").ap()
    cst_d = nc.dram_tensor("cst", [128, 5 * 128], F32, kind="ExternalInput").ap()
    y_d = nc.dram_tensor("y", [S, 1024], F32, kind="ExternalOutput").ap()
    dbg_d = None
    if debug_h:
        dbg_d = nc.dram_tensor("dbg", [len(debug_h), S, 1024], F32, kind="ExternalOutput").ap()
    wbf_d = nc.dram_tensor("wbf", [NSTREAM * 128, 1024], BF16, kind="Internal").ap()

    cx.new_epoch()

    def chunks(name, n, width, dtype):
        t = nc.alloc_sbuf_tensor(name, [128, n, width], dtype)
        full = t.ap()
        return full, [Buf("%s%d" % (name, c), full[:, c, :]) for c in range(n)]

    h_all, h = chunks("h", 8, T, F32)
    xn_all, xn = chunks("xn", 8, T, BF16)
    big_all, big = chunks("big", 24, T, BF16)
    ba_all, b8a = chunks("b8a", 8, T, BF16)
    bb_all, b8b = chunks("b8b", 8, T, BF16)
    kt_all, KT = chunks("kt", 8, 2 * T, BF16)
    vt_all, VT = chunks("vt", 8, 16 * 65, BF16)
    S32, Sbf = [], []
    for l in range(2):
        _, s32 = chunks("s32_%d" % l, 2, 512, F32)
        _, sb_ = chunks("sbf_%d" % l, 2, 512, BF16)
        S32.append(s32)
        Sbf.append(sb_)
    tailA = [chunks("tailA%d" % l, 24, 3, F32)[1] for l in range(2)]
    tailF = [chunks("tailF%d" % l, 44, 2, F32)[1] for l in range(4)]
    cst = cx.sb("cst_sb", [128, 5 * 128], F32)
    cstb = cx.sb("cstb", [128, 2 * 128], BF16)
    vecs = cx.sb("vecs_sb", [128, NV], F32)
    negA = cx.sb("negA", [128, 16], F32)
    ring = [cx.sb("ring%d" % i, [128, 1024], BF16) for i in range(NSLOT)]
    io32 = [cx.sb("io%d" % i, [128, 1024], F32) for i in range(2)]
    xc = [cx.sb("xc%d" % i, [128, T + 3], F32) for i in range(4)]
    acc = [cx.sb("acc%d" % i, [128, T], F32) for i in range(3)]
    tf = [cx.sb("tf%d" % i, [128, T], F32) for i in range(6)]
    tb = [cx.sb("tb%d" % i, [128, T], BF16) for i in range(35)]
    gsm = [cx.sb("gsm%d" % i, [128, 96 if i == 6 else 32], F32) for i in range(7)]
    rdt = cx.sb("rdt", [128, 8], F32)
    TSETS = [(tb[1:18], tf[2:4]), (tb[18:35], tf[4:6])]
    psb = [cx.ps("psb%d" % i, [128, 512], F32) for i in range(6)]
    pst = [cx.ps("pst%d" % i, [128, 1024], BF16) for i in range(2)]
    st = {"ps": 0, "pt": 0, "io": 0, "xc": 0}

    def psum():
        st["ps"] = (st["ps"] + 1) % len(psb)
        return psb[st["ps"]]

    def psumT():
        st["pt"] = (st["pt"] + 1) % len(pst)
        return pst[st["pt"]]

    def vcol(c, n=1):
        return vecs[:, c:c + n]

    ident = cst[:, C_ID:C_ID + 128]
    Umask = cst[:, C_U:C_U + 128]
    SLo = cst[:, C_SLO:C_SLO + 128]
    SU = cst[:, C_SU:C_SU + 128]
    ones = cst[:, C_ONE:C_ONE + 128]
    identb = cstb[:, 0:128]
    onesb = cstb[:, 128:256]

    d_misc = cx.dma_sem("misc")
    cx.dma("sync", cst.v, cst_d, d_misc)
    cx.dma("sync", vecs.v, vecs_d, d_misc)
    cx.op("dve", "tensor_copy", out=cstb[:, 0:128], in_=ident)
    cx.op("dve", "tensor_copy", out=cstb[:, 128:256], in_=ones)
    cx.op("act", "activation", out=negA.v, in_=vecs[:, V_ALOG:V_ALOG + 16], func=AF.Exp)
    cx.op("dve", "tensor_scalar", out=negA.v, in0=negA.v, scalar1=-1.0, scalar2=None, op0=ALU.mult)
    for l in range(2):
        for hg in range(2):
            cx.op("pool", "memset", ap=S32[l][hg].v, constant=0.0)
            cx.op("pool", "memset", ap=Sbf[l][hg].v, constant=0.0)
        for c in range(24):
            cx.op("pool", "memset", ap=tailA[l][c].v, constant=0.0)
    for l in range(4):
        for c in range(44):
            cx.op("pool", "memset", ap=tailF[l][c].v, constant=0.0)
    for bi in range(8):
        cx.op("pool", "memset", ap=VT[bi].v, constant=1.0)

    eb_slots = set()
    for j in range(2):
        for i in range(10):
            eb_slots.add(EB_SLOT[j] + i)
    groups = []
    slot_group = {}
    sidx = 0
    while sidx < NSTREAM:
        if sidx in eb_slots:
            n = 1
        else:
            n = 1
            while n < 8 and sidx + n < NSTREAM and (sidx + n) not in eb_slots:
                n += 1
        gb = Buf("wg%d" % sidx, wbf_d[sidx * 128:(sidx + n) * 128, :])
        groups.append((sidx, n, gb))
        for k in range(n):
            slot_group[sidx + k] = gb
        sidx += n
    NSEG = 6
    cast_groups = [g for g in groups if g[0] not in eb_slots]
    per = (len(cast_groups) + NSEG - 1) // NSEG
    for sg in range(NSEG):
        seg = cast_groups[sg * per:(sg + 1) * per]
        if not seg:
            continue
        d_cast = cx.dma_sem("cast%d" % sg)
        for (s0, n, gb) in seg:
            cx.dma("pool", gb.v, wst_d[s0 * 128:(s0 + n) * 128, :], d_cast)
        for (s0, n, gb) in seg:
            gb.w = (d_cast.sem, d_cast.count, "dma")
    ebt = [Buf("ebt%d" % i, big_all[:, 2 * i:2 * i + 2, :].rearrange("p a b -> p (a b)")) for i in range(2)]
    d_ebl = [cx.dma_sem("ebl%d" % i) for i in range(2)]
    d_ebs = cx.dma_sem("ebs")
    ebg = []
    for j in range(2):
        for i in range(10):
            iob = io32[i % 2]
            cx.dma("sync", iob.v, relb_d[j, :, i * 1024:(i + 1) * 1024], d_ebl[i % 2])
            ebtmp = tb[1 + 2 * (i % 2)] if False else ebt[i % 2]
            cx.op("act", "activation", out=ebtmp.v, in_=iob.v, func=AF.Exp)
            gb = slot_group[EB_SLOT[j] + i]
            cx.dma("sync", gb.v, ebtmp.v, d_ebs)
            ebg.append(gb)
    for gb in ebg:
        gb.w = (d_ebs.sem, d_ebs.count, "dma")

    ring_sem = [cx.dma_sem("ring%d" % i) for i in range(NSLOT)]
    rs = {"load": 0, "use": 0}
    total_uses = NT * NSTREAM

    def ring_prefetch(limit):
        while rs["load"] < min(limit, total_uses):
            L = rs["load"]
            sl = L % NSLOT
            s = L % NSTREAM
            gb = slot_group[s]
            cx.dma("sync", ring[sl].v, View(gb, wbf_d[s * 128:(s + 1) * 128, :]), ring_sem[sl])
            rs["load"] += 1

    def take():
        u = rs["use"]
        ring_prefetch(u + NSLOT - HOLD)
        rs["use"] += 1
        return ring[u % NSLOT]

    def wt(slot, k):
        return slot[:, k * 128:(k + 1) * 128]

    d_dump = cx.dma_sem("dump") if dumps else None
    dump_names = []

    def dump(name, view):
        if not dumps:
            return
        shp = list(view.ap.shape)
        cols = int(np.prod(shp[1:]))
        t = nc.dram_tensor("dd_" + name, [128, cols], F32, kind="ExternalOutput").ap()
        if len(shp) == 3:
            t = t.rearrange("p (a b) -> p a b", a=shp[1])
        cx.dma("pool", t, view, d_dump)
        dump_names.append(name)

    def rmsnorm(norm_idx, dst, f32_dst=None):
        for c in range(8):
            if c % 2 == 0:
                cx.op("act", "activation", out=xn[c].v, in_=h[c].v, func=AF.Square)
            else:
                cx.op("dve", "tensor_tensor", out=xn[c].v, in0=h[c].v, in1=h[c].v, op=ALU.mult)
        ps = psum()
        cx.mm(ps.v, [(onesb, xn[c].v) for c in range(8)])
        cx.op("act", "activation", out=tf[0].v, in_=ps.v, func=AF.Ln, scale=1.0 / 1024.0, bias=EPS)
        cx.op("act", "activation", out=tf[1].v, in_=tf[0].v, func=AF.Exp, scale=-0.5)
        for c in range(8):
            o = (f32_dst[c] if f32_dst is not None else dst[c])
            cx.op("dve", "scalar_tensor_tensor", out=o.v, in0=h[c].v, scalar=vcol(V_NORM + norm_idx * 8 + c),
                  in1=tf[1].v, op0=ALU.mult, op1=ALU.mult)

    def proj(slot, src):
        ps = psum()
        cx.mm(ps.v, [(wt(slot, kc), src[kc].v) for kc in range(8)])
        return ps

    def conv_taps(ps, tail, wbase, ntap, bias_col, accb):
        K = ntap - 1
        st["xc"] = (st["xc"] + 1) % len(xc)
        xb = xc[st["xc"]]
        cx.op("pool", "tensor_copy", out=xb[:, 0:K], in_=tail.v)
        cx.op("act", "activation", out=xb[:, K:K + T], in_=ps.v, func=AF.Copy)
        cx.op("pool", "tensor_copy", out=tail.v, in_=xb[:, T:T + K])
        if bias_col is None:
            cx.op("act", "activation", out=accb.v, in_=xb[:, 0:T], func=AF.Identity, scale=vcol(wbase))
        else:
            cx.op("act", "activation", out=accb.v, in_=xb[:, 0:T], func=AF.Identity, scale=vcol(wbase), bias=vcol(bias_col))
        for k in range(1, ntap):
            cx.op("dve", "scalar_tensor_tensor", out=accb.v, in0=xb[:, k:k + T], scalar=vcol(wbase + k),
                  in1=accb.v, op0=ALU.mult, op1=ALU.add)

    def residual_out(slots_fn, src, nk):
        for oc in range(8):
            ps = psum()
            cx.mm(ps.v, [(slots_fn(oc, kc), src[kc].v) for kc in range(nk)])
            cx.op("dve", "tensor_tensor", out=h[oc].v, in0=ps.v, in1=h[oc].v, op=ALU.add)

    def ffn(l):
        rmsnorm(N_FNORM + l, xn)
        for c in range(22):
            sg = take()
            psg = proj(sg, xn)
            conv_taps(psg, tailF[l][c], V_FCONV + l * 132 + c * 3, 3, V_FBIAS + l * 44 + c, acc[0])
            sv = take()
            psv = proj(sv, xn)
            conv_taps(psv, tailF[l][22 + c], V_FCONV + l * 132 + (22 + c) * 3, 3, V_FBIAS + l * 44 + 22 + c, acc[1])
            cx.op("act", "activation", out=acc[2].v, in_=acc[0].v, func=AF.Silu)
            cx.op("dve", "tensor_tensor", out=big[c].v, in0=acc[1].v, in1=acc[2].v, op=ALU.mult)
        held = {}

        def dtile(oc, kc):
            idx = oc * 22 + kc
            si = idx // 8
            while si not in held:
                held[len(held)] = take()
            return wt(held[si], idx % 8)
        residual_out(dtile, big, 22)
        while len(held) < 22:
            held[len(held)] = take()

    def a_layer(l, ti):
        rmsnorm(N_ANORM + l, xn)
        pend = None
        for oc in range(24):
            sl = take()
            ps = proj(sl, xn)
            conv_taps(ps, tailA[l][oc], V_ACONV + l * 96 + oc * 4, 4, None, acc[oc % 3])
            if pend is not None:
                cx.op("act", "activation", out=big[pend].v, in_=acc[pend % 3].v, func=AF.Silu)
            pend = oc
        cx.op("act", "activation", out=big[pend].v, in_=acc[pend % 3].v, func=AF.Silu)
        for oc in range(8):
            sl = take()
            ps = proj(sl, xn)
            cx.op("act", "activation", out=b8a[oc].v, in_=ps.v, func=AF.Silu)
        def l2_a(c):
            sq = tb[0] if c % 2 == 0 else tb[1]
            if c % 2 == 0:
                cx.op("act", "activation", out=sq.v, in_=big[c].v, func=AF.Square)
            else:
                cx.op("dve", "tensor_tensor", out=sq.v, in0=big[c].v, in1=big[c].v, op=ALU.mult)
            ps = psum()
            cx.mm(ps.v, [(onesb, sq.v)])
            return ps

        def l2_b(c, ps):
            ta, tb_ = (tf[0], tf[1]) if c % 2 == 0 else (tf[2], tf[3])
            cx.op("act", "activation", out=ta.v, in_=ps.v, func=AF.Ln, bias=EPS)
            cx.op("act", "activation", out=tb_.v, in_=ta.v, func=AF.Exp, scale=-0.5,
                  bias=(float(np.log(128.0 ** -0.5)) if c < 8 else 0.0))
            cx.op("dve", "tensor_tensor", out=big[c].v, in0=big[c].v, in1=tb_.v, op=ALU.mult)
        pend = None
        for c in range(16):
            ps = l2_a(c)
            if pend is not None:
                l2_b(*pend)
            pend = (c, ps)
        l2_b(*pend)
        sl_ba = take()
        gdn_all(l, sl_ba)
        def on_a(hd):
            sq = tb[0] if hd % 2 == 0 else tb[1]
            if hd % 2 == 0:
                cx.op("act", "activation", out=sq.v, in_=b8b[hd].v, func=AF.Square)
            else:
                cx.op("dve", "tensor_tensor", out=sq.v, in0=b8b[hd].v, in1=b8b[hd].v, op=ALU.mult)
            ps = psum()
            cx.mm(ps.v, [(onesb, sq.v)])
            return ps

        def on_b(hd, ps):
            ta, tb_, tc = (tf[0], tf[1], tf[4]) if hd % 2 == 0 else (tf[2], tf[3], tf[5])
            cx.op("act", "activation", out=ta.v, in_=ps.v, func=AF.Ln, scale=1.0 / 128.0, bias=EPS)
            cx.op("act", "activation", out=tb_.v, in_=ta.v, func=AF.Exp, scale=-0.5)
            cx.op("dve", "tensor_tensor", out=tc.v, in0=b8b[hd].v, in1=tb_.v, op=ALU.mult)
            cx.op("dve", "scalar_tensor_tensor", out=b8b[hd].v, in0=tc.v, scalar=vcol(V_ONW + l),
                  in1=b8a[hd].v, op0=ALU.mult, op1=ALU.mult)
        pend = None
        for hd in range(8):
            ps = on_a(hd)
            if pend is not None:
                on_b(*pend)
            pend = (hd, ps)
        on_b(*pend)
        for oc in range(8):
            sl = take()
            ps = psum()
            cx.mm(ps.v, [(wt(sl, kc), b8b[kc].v) for kc in range(8)])
            cx.op("dve", "tensor_tensor", out=h[oc].v, in0=ps.v, in1=h[oc].v, op=ALU.add)

    def gdn_gates(l, sl_ba, G):
        g_eb, g_beta, g_xa, g_ax, g_la, g_g, g_e3 = G
        psg = psum()
        ins = []
        for blk in range(4):
            for kc in range(8):
                ins.append((psg[:, blk * 16:(blk + 1) * 16], xn[kc][:, blk * 128:(blk + 1) * 128], sl_ba[:, kc * 16:(kc + 1) * 16],
                            kc == 0, kc == 7, False))
        cx.pe([psg], ins)
        psg3 = View(psg, psg.ap[:, 0:64].rearrange("p (b e) -> p b e", e=16))

        def v3(tile):
            return View(tile, tile.ap[:, 0:32].rearrange("p (b e) -> p b e", e=8))
        cx.op("act", "activation", out=v3(g_eb), in_=psg3[:, :, 0:8], func=AF.Exp, scale=-1.0)
        cx.op("dve", "tensor_scalar", out=g_eb[:, 0:32], in0=g_eb[:, 0:32], scalar1=1.0, scalar2=None, op0=ALU.add)
        cx.op("dve", "reciprocal", out=g_beta[:, 0:32], in_=g_eb[:, 0:32])
        cx.op("dve", "tensor_tensor", out=v3(g_xa), in0=psg3[:, :, 8:16],
              in1=_bc(vecs[:, V_DTB + l * 8:V_DTB + l * 8 + 8], [128, 4, 8], 1), op=ALU.add)
        cx.op("dve", "scalar_tensor_tensor", out=g_ax[:, 0:32], in0=g_xa[:, 0:32], scalar=-1.0, in1=g_xa[:, 0:32],
              op0=ALU.mult, op1=ALU.max)
        cx.op("act", "activation", out=g_la[:, 0:32], in_=g_ax[:, 0:32], func=AF.Exp, scale=-1.0)
        cx.op("act", "activation", out=g_la[:, 0:32], in_=g_la[:, 0:32], func=AF.Ln, bias=1.0)
        cx.op("dve", "scalar_tensor_tensor", out=g_g[:, 0:32], in0=g_xa[:, 0:32], scalar=0.0, in1=g_la[:, 0:32],
              op0=ALU.max, op1=ALU.add)
        cx.op("dve", "tensor_tensor", out=v3(g_g), in0=v3(g_g), in1=_bc(negA[:, l * 8:l * 8 + 8], [128, 4, 8], 1), op=ALU.mult)
        psc = psum()
        cx.pe([psc], [(psc[:, 0:32], Umask, g_g[:, 0:32], True, True, False),
                      (psc[:, 32:64], SLo, g_g[:, 0:32], True, True, False),
                      (psc[:, 64:96], ones, g_g[:, 0:32], True, True, False)])
        cx.op("act", "activation", out=g_e3[:, 0:96], in_=psc[:, 0:96], func=AF.Exp)

    def gdn_unit(l, blk, hg, tok, G, TS):
        g_eb, g_beta, g_xa, g_ax, g_la, g_g, g_e3 = G
        tbs, tfs = TS

        def D(name, view):
            if l == DUMP_L and blk < 2 and st.get("ti", 0) == 0 and hg == DUMP_HG:
                dump("%s_b%d" % (name, blk), view)

        def hv(bufv, i):
            return bufv[:, i * 128:(i + 1) * 128]

        def bcast_gate(tile, col0):
            c0 = (col0 // 8) * 32 + blk * 8 + hg * 4
            return _bc(tile[:, c0:c0 + 4], [128, 4, 128], 2)

        def as3(v):
            return View(v.buf, v.ap.rearrange("p (a b) -> p a b", a=4))

        if hg == DUMP_HG:
            D("beta", g_beta[:, blk * 8:blk * 8 + 8]); D("g", g_g[:, blk * 8:blk * 8 + 8])
        my_psb = psb[3 * hg:3 * hg + 3]
        cnt = {"n": 0}

        def psum():
            cnt["n"] += 1
            return my_psb[cnt["n"] % 3]

        def psumT():
            return pst[hg]
        kT_v = View(big[8 + hg * 4], big_all[:, 8 + hg * 4:12 + hg * 4, tok])
        qT_v = View(big[hg * 4], big_all[:, hg * 4:hg * 4 + 4, tok])
        qbufs = [big[hg * 4 + i] for i in range(4)]
        (K_eg, K_e, Vt, DTc, DTs, N2, attnT, M2, PA, QA, RA, PB, QB, RB, W2n, QdT, vnew) = tbs
        GU, egrow = tfs
        Ex = GU
        pk = psumT()
        cx.pe([pk], [(hv(pk.v, i), big[8 + hg * 4 + i][:, tok], identb, True, True, True) for i in range(4)])
        cx.op("pool", "tensor_tensor", out=as3(GU.v), in0=_bc(Umask, [128, 4, 128], 1), in1=bcast_gate(g_g, 0), op=ALU.mult)
        pD = psum()
        cx.mm(pD.v, [(SLo, GU.v)])
        pR = psum()
        cx.mm(pR.v, [(ones, GU.v)])
        yield
        cx.op("dve", "tensor_tensor", out=as3(K_eg.v), in0=as3(pk[:, 0:512]), in1=bcast_gate(g_e3, 0), op=ALU.mult)
        cx.op("dve", "tensor_tensor", out=as3(K_e.v), in0=as3(pk[:, 0:512]), in1=bcast_gate(g_e3, 8), op=ALU.mult)
        cx.op("act", "activation", out=Ex.v, in_=pD.v, func=AF.Exp)
        cx.op("act", "activation", out=egrow.v, in_=pR.v, func=AF.Exp)
        cx.op("pool", "tensor_tensor", out=as3(DTs.v), in0=as3(Ex.v), in1=_bc(SU, [128, 4, 128], 1), op=ALU.mult)
        cx.op("pool", "tensor_tensor", out=as3(DTc.v), in0=as3(Ex.v), in1=_bc(Umask, [128, 4, 128], 1), op=ALU.mult)
        cx.op("pool", "tensor_tensor", out=as3(DTs.v), in0=as3(DTs.v), in1=bcast_gate(g_beta, 0), op=ALU.mult)
        pv = psumT()
        cx.pe([pv], [(hv(pv.v, i), big[16 + hg * 4 + i][:, tok], identb, True, True, True) for i in range(4)])
        pA1 = psum()
        cx.pe([pA1], [(hv(pA1.v, i), big[8 + hg * 4 + i][:, tok], big[8 + hg * 4 + i][:, tok], True, True, False) for i in range(4)])
        pA2 = psum()
        cx.pe([pA2], [(hv(pA2.v, i), big[8 + hg * 4 + i][:, tok], big[hg * 4 + i][:, tok], True, True, False) for i in range(4)])
        yield
        cx.op("act", "activation", out=Vt.v, in_=pv[:, 0:512], func=AF.Copy)
        cx.op("dve", "tensor_tensor", out=N2.v, in0=pA1.v, in1=DTs.v, op=ALU.mult)
        cx.op("dve", "tensor_tensor", out=attnT.v, in0=pA2.v, in1=DTc.v, op=ALU.mult)
        D("K_eg", K_eg.v); D("K_e", K_e.v); D("Vt", Vt.v); D("Ex", Ex.v); D("egrow", egrow.v)
        D("DTc", DTc.v); D("N2", N2.v); D("attnT", attnT.v)
        pm = psumT()
        cx.pe([pm], [(hv(pm.v, i), hv(N2.v, i), identb, True, True, True) for i in range(4)])
        yield
        cx.op("act", "activation", out=M2.v, in_=pm[:, 0:512], func=AF.Copy)
        D("M2", M2.v)
        cx.op("pool", "scalar_tensor_tensor" if False else "tensor_tensor", out=as3(RA.v), in0=_bc(identb, [128, 4, 128], 1), in1=as3(N2.v), op=ALU.subtract)
        p2 = psum()
        cx.pe([p2], [(hv(p2.v, i), hv(M2.v, i), hv(N2.v, i), True, True, False) for i in range(4)])
        pq = psum()
        cx.pe([pq], [(hv(pq.v, i), hv(N2.v, i), hv(M2.v, i), True, True, False) for i in range(4)])
        yield
        P, Q, R = PB, QB, RA
        cx.op("dve", "tensor_copy", out=P.v, in_=p2.v)
        cx.op("act", "activation", out=Q.v, in_=pq.v, func=AF.Copy)
        sets = [(PA, QA, RB), (PB, QB, RA)]
        for k in range(1, 7):
            Pn, Qn, Rn = sets[(k - 1) % 2]
            pr = psum()
            cx.pe([pr], [(hv(pr.v, i), hv(Q.v, i), hv(R.v, i), True, True, False) for i in range(4)])
            if k < 6:
                p2 = psum()
                cx.pe([p2], [(hv(p2.v, i), hv(Q.v, i), hv(P.v, i), True, True, False) for i in range(4)])
                pq = psum()
                cx.pe([pq], [(hv(pq.v, i), hv(P.v, i), hv(Q.v, i), True, True, False) for i in range(4)])
            yield
            cx.op("dve", "tensor_tensor", out=Rn.v, in0=pr.v, in1=R.v, op=ALU.add)
            if k < 6:
                cx.op("act", "activation", out=Qn.v, in_=pq.v, func=AF.Copy)
                cx.op("dve", "tensor_copy", out=Pn.v, in_=p2.v)
            P, Q, R = Pn, Qn, Rn
        spare = [PA, QA, PB, QB]
        for it in range(N_NEWTON):
            XT, En = spare[0], spare[1]
            Xn = spare[2 + it % 2]
            pX = psumT()
            cx.pe([pX], [(hv(pX.v, i), hv(R.v, i), identb, True, True, True) for i in range(4)])
            pE = psum()
            ins = []
            for i in range(4):
                ins.append((hv(pE.v, i), hv(M2.v, i), hv(R.v, i), True, False, False))
                ins.append((hv(pE.v, i), identb, hv(R.v, i), False, True, False))
            cx.pe([pE], ins)
            yield
            cx.op("act", "activation", out=XT.v, in_=pX[:, 0:512], func=AF.Copy)
            cx.op("dve", "scalar_tensor_tensor", out=as3(En.v), in0=as3(pE.v), scalar=-1.0,
                  in1=_bc(identb, [128, 4, 128], 1), op0=ALU.mult, op1=ALU.add)
            pN = psum()
            cx.pe([pN], [(hv(pN.v, i), hv(XT.v, i), hv(En.v, i), True, True, False) for i in range(4)])
            yield
            cx.op("dve", "tensor_tensor", out=Xn.v, in0=pN.v, in1=R.v, op=ALU.add)
            R = Xn
        T2T = R
        D("T2T", T2T.v)
        pw = psum()
        cx.pe([pw], [(hv(pw.v, i), hv(K_eg.v, i), hv(T2T.v, i), True, True, False) for i in range(4)])
        cx.op("dve", "tensor_tensor", out=as3(QdT.v), in0=qT_v, in1=as3(egrow.v), op=ALU.mult, _reads=qbufs)
        yield
        cx.op("act", "activation", out=W2n.v, in_=pw.v, func=AF.Copy, scale=-1.0)
        D("W2n", W2n.v); D("QdT", QdT.v)
        Sb = Sbf[l][hg]
        S3 = S32[l][hg]
        pV = psum()
        ins = []
        for i in range(4):
            ins.append((hv(pV.v, i), hv(T2T.v, i), hv(Vt.v, i), True, False, False))
            ins.append((hv(pV.v, i), hv(W2n.v, i), hv(Sb.v, i), False, True, False))
        cx.pe([pV], ins)
        yield
        cx.op("dve", "tensor_tensor", out=as3(vnew.v), in0=as3(pV.v), in1=bcast_gate(g_beta, 0), op=ALU.mult)
        D("vnew", vnew.v)
        pO = psum()
        ins = []
        for i in range(4):
            ins.append((hv(pO.v, i), hv(Sb.v, i), hv(QdT.v, i), True, False, False))
            ins.append((hv(pO.v, i), hv(vnew.v, i), hv(attnT.v, i), False, True, False))
        cx.pe([pO], ins)
        pS = psum()
        cx.pe([pS], [(hv(pS.v, i), hv(K_e.v, i), hv(vnew.v, i), True, True, False) for i in range(4)])
        cx.op("pool", "tensor_tensor", out=as3(S3.v), in0=as3(S3.v), in1=bcast_gate(g_e3, 16), op=ALU.mult)
        yield
        obufs = [b8b[hg * 4 + i] for i in range(4)]
        cx.op("act", "activation", out=View(obufs[0], bb_all[:, hg * 4:hg * 4 + 4, tok]), in_=as3(pO.v),
              func=AF.Copy, _writes=obufs[1:])
        cx.op("dve", "tensor_tensor", out=S3.v, in0=pS.v, in1=S3.v, op=ALU.add)
        cx.op("act", "activation", out=Sb.v, in_=S3.v, func=AF.Copy)
        D("S3", S3.v); D("oT", View(obufs[0], bb_all[:, hg * 4:hg * 4 + 4, tok])); D("qT", qT_v); D("kT", kT_v)

    def gdn_all(l, sl_ba):
        toks = [slice(b_ * 128, (b_ + 1) * 128) for b_ in range(4)]
        gdn_gates(l, sl_ba, gsm)

        def stream(hg):
            for blk in range(4):
                yield from gdn_unit(l, blk, hg, toks[blk], gsm, TSETS[hg])
        gens = [stream(0), stream(1)]
        while gens:
            for g in list(gens):
                try:
                    next(g)
                except StopIteration:
                    gens.remove(g)

    def kv_phase(ti):
        par = ti % 2
        rmsnorm(N_KVNORM, xn)
        for oc in range(8):
            sl = take()
            ps = proj(sl, xn)
            cx.op("act", "activation", out=KT[oc][:, par * T:(par + 1) * T], in_=ps.v, func=AF.Copy)
        held = [take() for _ in range(4)]
        for cg in range(2):
            if cg == 1:
                held = [take() for _ in range(4)]
            for blk in range(4):
                ps = psum()
                pairs = []
                for kc in range(8):
                    idx = kc
                    pairs.append((xn[kc][:, blk * 128:(blk + 1) * 128], held[idx // 2][:, (idx % 2) * 512:(idx % 2) * 512 + 512]))
                cx.mm(ps.v, pairs)
                vout = View(VT[par * 4 + blk], vt_all[:, par * 4 + blk, :].rearrange("p (h e) -> p h e", e=65)[:, cg * 8:(cg + 1) * 8, 0:64])
                vin = View(ps, ps.ap.rearrange("p (h e) -> p h e", e=64))
                if blk % 2 == 0:
                    cx.op("act", "activation", out=vout, in_=vin, func=AF.Copy)
                else:
                    cx.op("dve", "tensor_copy", out=vout, in_=vin)

    def b_layer(j, ti):
        par = ti % 2
        rmsnorm(N_BNORM + j, xn)
        for oc in range(8):
            sl = take()
            ps = proj(sl, xn)
            qa, qb = big[8 + 2 * oc], big[9 + 2 * oc]
            cx.op("pool", "memset", ap=qa[64:128, :], constant=0.0)
            cx.op("pool", "memset", ap=qb[0:64, :], constant=0.0)
            cx.op("act", "activation", out=qa[0:64, :], in_=ps[0:64, :], func=AF.Copy)
            cx.op("dve", "tensor_copy", out=qb[64:128, :], in_=ps[64:128, :])
        eb = {}

        def att_a(hg, m):
            it_ = hg * 4 + m
            for kb in range(5):
                si = (hg * 5 + kb) // 2
                while si not in eb:
                    eb[len(eb)] = take()
            qs = slice(m * 128, (m + 1) * 128)
            kbs = [kb for kb in range(5) if not (ti == 0 and m + kb - 4 < 0)]
            PT = {}
            for n_, kb in enumerate(kbs):
                rel = m + kb - 4
                if rel < 0:
                    kpar, kblk = 1 - par, 4 + rel
                else:
                    kpar, kblk = par, rel
                kcols = slice(kpar * T + kblk * 128, kpar * T + kblk * 128 + 128)
                pS = psum()
                ins = []
                for hh in range(4):
                    head = hg * 4 + hh
                    ins.append((pS[:, hh * 128:(hh + 1) * 128], KT[head // 2][:, kcols], big[8 + head][:, qs], True, True, False))
                cx.pe([pS], ins)
                ex = tf[2 + (it_ * 5 + n_) % 4]
                cx.op("act", "activation", out=ex.v, in_=pS.v, func=AF.Exp, scale=0.125)
                idx = hg * 5 + kb
                ebv = eb[idx // 2][:, (idx % 2) * 512:(idx % 2) * 512 + 512]
                pt = tb[(it_ % 2) * 5 + n_]
                cx.op("pool" if n_ % 2 == 1 else "dve", "tensor_tensor", out=pt.v, in0=ex.v, in1=ebv, op=ALU.mult)
                PT[kb] = (pt, kpar * 4 + kblk)
            return (hg, m, kbs, PT)

        def att_b(hg, m, kbs, PT):
            pO = psb[6 + (hg * 4 + m) % 2] if len(psb) > 6 else psum()
            ins = []
            for hh in range(4):
                head = hg * 4 + hh
                for n_, kb in enumerate(kbs):
                    pt, vb = PT[kb]
                    ins.append((pO[:, hh * 65:(hh + 1) * 65], pt[:, hh * 128:(hh + 1) * 128],
                                VT[vb][:, head * 65:(head + 1) * 65], n_ == 0, n_ == len(kbs) - 1, False))
            cx.pe([pO], ins)
            pO3 = View(pO, pO.ap[:, 0:260].rearrange("p (h e) -> p h e", e=65))
            rd = rdt
            cx.op("dve", "reciprocal", out=rd[:, 0:4], in_=pO3[:, :, 64])
            otm = View(big[2 * m], big_all[:, 2 * m:2 * m + 2, :].rearrange("p a b -> p (a b)")[:, hg * 256:(hg + 1) * 256]
                       .rearrange("p (h e) -> p h e", e=64))
            cx.op("dve", "tensor_tensor", out=otm, in0=pO3[:, :, 0:64], in1=_bc(rd[:, 0:4], [128, 4, 64], 2),
                  op=ALU.mult, _writes=[big[2 * m + 1]])
        pend = None
        for hg in range(4):
            for m in range(4):
                cur = att_a(hg, m)
                if pend is not None:
                    att_b(*pend)
                pend = cur
        att_b(*pend)
        for m in range(4):
            qs = slice(m * 128, (m + 1) * 128)
            otm2 = big_all[:, 2 * m:2 * m + 2, :].rearrange("p a b -> p (a b)")
            pT = psumT()
            cx.pe([pT], [(pT[:, c * 128:(c + 1) * 128], View(big[2 * m + c // 4], otm2[:, c * 128:(c + 1) * 128]), identb, True, True, True)
                         for c in range(8)])
            cx.op("act" if m % 2 == 0 else "dve", "activation" if m % 2 == 0 else "tensor_copy",
                  out=View(b8b[0], bb_all[:, 0:8, qs]), in_=View(pT, pT.ap.rearrange("p (a b) -> p a b", a=8)),
                  _writes=b8b[1:], **({"func": AF.Copy} if m % 2 == 0 else {}))
        while len(eb) < 10:
            eb[len(eb)] = take()
        for oc in range(8):
            sl = take()
            ps = psum()
            cx.mm(ps.v, [(wt(sl, kc), b8b[kc].v) for kc in range(8)])
            cx.op("dve", "tensor_tensor", out=h[oc].v, in0=ps.v, in1=h[oc].v, op=ALU.add)

    d_in = [cx.dma_sem("in%d" % i) for i in range(2)]
    d_out = [cx.dma_sem("out%d" % i) for i in range(2)]
    d_dbg = [cx.dma_sem("dbg%d" % i) for i in range(2)] if debug_h else []
    io_n = {"i": 0}

    d_pre = [cx.dma_sem("pre%d" % i) for i in range(2)]
    pre_state = {"tile": -1}

    def prefetch_x(ti):
        if ti >= NT:
            return
        for blk in range(2):
            r0 = ti * T + blk * 128
            for half in range(2):
                cx.dma("sync", tf[2 + blk * 2 + half].v, x_d[r0:r0 + 128, half * 512:(half + 1) * 512], d_pre[blk])
            for half in range(2):
                tf[2 + blk * 2 + half].w = (d_pre[blk].sem, d_pre[blk].count, "dma")
        pre_state["tile"] = ti

    def load_tile(ti):
        for blk in range(4):
            pre = (pre_state["tile"] == ti and blk < 2)
            if not pre:
                k = io_n["i"] % 2
                io_n["i"] += 1
                iob = io32[k]
                r0 = ti * T + blk * 128
                cx.dma("sync", iob.v, x_d[r0:r0 + 128, :], d_in[k])
            for half in range(2):
                ps = psum()
                if pre:
                    src = tf[2 + blk * 2 + half]
                    cx.pe([ps], [(ps[:, i * 128:(i + 1) * 128], src[:, i * 128:(i + 1) * 128], ident, True, True, True)
                                 for i in range(4)])
                else:
                    cx.pe([ps], [(ps[:, i * 128:(i + 1) * 128], iob[:, (half * 4 + i) * 128:(half * 4 + i + 1) * 128], ident, True, True, True)
                                 for i in range(4)])
                hb = [h[half * 4 + i] for i in range(4)]
                cx.op("act" if half == 0 else "dve", "activation" if half == 0 else "tensor_copy",
                      out=View(hb[0], h_all[:, half * 4:half * 4 + 4, blk * 128:(blk + 1) * 128]),
                      in_=View(ps, ps.ap.rearrange("p (a b) -> p a b", a=4)), _writes=hb[1:],
                      **({"func": AF.Copy} if half == 0 else {}))

    def store_tile(src_chunks, dst_d, ti, sems):
        for blk in range(4):
            k = io_n["i"] % 2
            io_n["i"] += 1
            iob = io32[k]
            for half in range(2):
                ps = psum()
                cx.pe([ps], [(ps[:, i * 128:(i + 1) * 128], src_chunks[half * 4 + i][:, blk * 128:(blk + 1) * 128], ident, True, True, True)
                             for i in range(4)])
                if half == 0:
                    cx.op("act", "activation", out=iob[:, 0:512], in_=ps.v, func=AF.Copy)
                else:
                    cx.op("dve", "tensor_copy", out=iob[:, 512:1024], in_=ps.v)
            r0 = ti * T + blk * 128
            cx.dma("sync", dst_d[r0:r0 + 128, :], iob.v, sems[k])

    def dbg_dump(tag, ti):
        if tag in debug_h:
            store_tile(h, dbg_d[debug_h.index(tag)], ti, d_dbg)

    def skip(n):
        for _ in range(n):
            take()

    for ti in range(NT):
        if ti > 0:
            cx.new_epoch()
        st["ti"] = ti
        load_tile(ti)
        for l in range(2):
            if ("a%d" % l) in phases:
                a_layer(l, ti)
            else:
                skip(N_A)
            dbg_dump("a%d" % l, ti)
            if ("f%d" % l) in phases:
                ffn(l)
            else:
                skip(N_F)
            dbg_dump("f%d" % l, ti)
        if "kv" in phases:
            kv_phase(ti)
        else:
            skip(N_KV)
        for j in range(2):
            if ("b%d" % j) in phases:
                b_layer(j, ti)
            else:
                skip(N_B)
            dbg_dump("b%d" % j, ti)
            if j == 1:
                prefetch_x(ti + 1)
            if ("f%d" % (2 + j)) in phases:
                ffn(2 + j)
            else:
                skip(N_F)
            dbg_dump("f%d" % (2 + j), ti)
        if "fin" in phases:
            rmsnorm(N_FINAL, None, f32_dst=h)
        store_tile(h, y_d, ti, d_out)
    cx.finish(d_out + d_dbg + ([d_dump] if dumps else []))
    assert rs["use"] == total_uses, (rs["use"], total_uses)
    return nc, cx


def _proj_slots(W, ocs):
    out = []
    for oc in ocs:
        blk = W[:, oc * 128:(oc + 1) * 128].reshape(8, 128, 128)
        out.append(blk.transpose(1, 0, 2).reshape(128, 1024))
    return out


def pack_wstream(inp):
    f32 = np.float32
    slots = []

    def ffn(l):
        Wup = inp["f_w_up"][l]
        order = [c for cc in range(22) for c in (cc, 22 + cc)]
        slots.extend(_proj_slots(Wup, order))
        Wd = inp["f_w_down"][l]
        tiles = Wd.reshape(22, 128, 8, 128).transpose(2, 0, 1, 3).reshape(176, 128, 128)
        sl = tiles.reshape(22, 8, 128, 128).transpose(0, 2, 1, 3).reshape(22, 128, 1024)
        slots.extend(list(sl))

    for l in range(2):
        Win = inp["a_w_in"][l]
        slots.extend(_proj_slots(Win, range(32)))
        ba = np.zeros((128, 1024), f32)
        for kc in range(8):
            ba[:, kc * 16:(kc + 1) * 16] = Win[kc * 128:(kc + 1) * 128, 4096:4112]
        slots.append(ba)
        slots.extend(_proj_slots(inp["a_w_out"][l], range(8)))
        ffn(l)
    wkv = inp["w_kv"]
    slots.extend(_proj_slots(wkv, range(8)))
    vs = np.zeros((8, 128, 1024), f32)
    for idx in range(16):
        cg, kc = divmod(idx, 8)
        vs[idx // 2][:, (idx % 2) * 512:(idx % 2) * 512 + 512] = wkv[kc * 128:(kc + 1) * 128, 1024 + cg * 512:1024 + (cg + 1) * 512]
    slots.extend(list(vs))
    for j in range(2):
        slots.extend(_proj_slots(inp["b_w_q"][j], range(8)))
        slots.extend([np.zeros((128, 1024), f32)] * 10)
        slots.extend(_proj_slots(inp["b_w_out"][j], range(8)))
        ffn(2 + j)
    assert len(slots) == NSTREAM, len(slots)
    return np.ascontiguousarray(np.stack(slots, 0).reshape(NSTREAM * 128, 1024).astype(f32))


def pack_relb(rel_bias):
    out = np.zeros((2, 128, 80 * 128), np.float32)
    k = np.arange(128)[:, None]
    q = np.arange(128)[None, :]
    for j in range(2):
        for hg in range(4):
            for kb in range(5):
                rel = np.clip(q + 512 - k - 128 * kb, -256, 256) + 256
                for hh in range(4):
                    tile = rel_bias[j, hg * 4 + hh][rel].astype(np.float32)
                    if kb == 0:
                        tile[:64, 64:] = -30000.0
                    if kb == 4:
                        tile[64:, :64] = -30000.0
                    t_idx = (hg * 5 + kb) * 4 + hh
                    out[j, :, t_idx * 128:(t_idx + 1) * 128] = tile
    return out


def pack_vecs(inp):
    v = np.zeros((128, NV), np.float32)

    def ch(x):
        return np.asarray(x, np.float32).reshape(-1, 128).T
    norms = [inp["a_norm"][0], inp["a_norm"][1], inp["kv_norm"], inp["b_norm"][0], inp["b_norm"][1],
             inp["f_norm"][0], inp["f_norm"][1], inp["f_norm"][2], inp["f_norm"][3], inp["final_norm"]]
    for i, g in enumerate(norms):
        v[:, V_NORM + i * 8:V_NORM + i * 8 + 8] = ch(g)
    for l in range(2):
        w = inp["a_conv"][l]
        blk = np.stack([ch(w[k]) for k in range(4)], -1)
        v[:, V_ACONV + l * 96:V_ACONV + (l + 1) * 96] = blk.reshape(128, 96)
        v[:, V_ONW + l] = inp["a_out_norm"][l]
        v[:, V_ALOG + l * 8:V_ALOG + l * 8 + 8] = inp["a_A_log"][l][None, :]
        v[:, V_DTB + l * 8:V_DTB + l * 8 + 8] = inp["a_dt_bias"][l][None, :]
    for l in range(4):
        w = inp["f_conv"][l]
        blk = np.stack([ch(w[k]) for k in range(3)], -1)
        v[:, V_FCONV + l * 132:V_FCONV + (l + 1) * 132] = blk.reshape(128, 132)
        v[:, V_FBIAS + l * 44:V_FBIAS + (l + 1) * 44] = ch(inp["f_conv_b"][l])
    return v


def pack_cst():
    p = np.arange(128)[:, None]
    f = np.arange(128)[None, :]
    c = np.zeros((128, 5 * 128), np.float32)
    c[:, C_ID:C_ID + 128] = (p == f)
    c[:, C_U:C_U + 128] = (p <= f)
    c[:, C_SLO:C_SLO + 128] = (p > f)
    c[:, C_SU:C_SU + 128] = (f > p)
    c[:, C_ONE:C_ONE + 128] = 1.0
    return c


N_CORES = 8


def kernel(**inputs):
    inp = {k: np.asarray(v) for k, v in inputs.items()}
    x = inp["x"]
    B = x.shape[0]
    nc, _ = build_program(NT_FULL)
    wst = pack_wstream(inp)
    relb = pack_relb(inp["b_rel_bias"])
    vecs = pack_vecs(inp)
    cst = pack_cst()
    in_maps = []
    for c in range(N_CORES):
        in_maps.append({"x": np.ascontiguousarray(x[c % B]), "wst": wst, "relb": relb, "vecs": vecs, "cst": cst})
    res = run_bass_kernel_spmd(nc, in_maps, core_ids=list(range(N_CORES)))
    out = np.stack([np.asarray(res.results[b]["y"]) for b in range(B)], 0)
    return out.astype(np.float32)
```
